# Optimizing a Trainium2 kernel written in Bass

```python
import math
import numpy as np
import jax
import jax.numpy as jnp
from jax import lax

D_MODEL = 1024
BATCH = 8
SEQ = 4096
DEPTH = 1

EPS = 1e-6
NEG_INF = -1e30

N_HEADS = 16
N_KV_GROUPS = 2
HEADS_PER_GROUP = N_HEADS // N_KV_GROUPS
HEAD_DIM = 64
ROT_DIM = HEAD_DIM // 4
ROPE_THETA = 500000.0
CMP_BLOCK = 32
CMP_STRIDE = 16
CMP_HIDDEN = 256
SLC_BLOCK = 64
SLC_TOPK = 16
WINDOW = 512
Q_BLOCK = 64

SSM_INNER = 2 * D_MODEL
SSM_HEAD_DIM = 64
SSM_HEADS = SSM_INNER // SSM_HEAD_DIM
SSM_GROUPS = 4
SSM_STATE = 128
SSM_CONV = 4
SSM_CHUNK = 128
DT_MIN = 1e-3
DT_MAX = 1e-1
DT_PROJ_SCALE = 0.1

D_FF = -(-8 * D_MODEL // (3 * 256)) * 256

Q_W = N_HEADS * HEAD_DIM
KV_W = N_KV_GROUPS * HEAD_DIM
XBC_W = SSM_INNER + 2 * SSM_GROUPS * SSM_STATE
IN_WIDTHS = (Q_W, 6 * KV_W, 3 * N_HEADS, SSM_INNER, XBC_W, SSM_HEADS, 2 * D_MODEL)
IN_TOTAL = sum(IN_WIDTHS)
IN_OFFSETS = tuple(int(o) for o in np.cumsum(IN_WIDTHS)[:-1])

kernel_name = 'hybrid_nsa_mamba2_adaln_block'


def _rms(x):
    xf = x.astype(jnp.float32)
    return xf * lax.rsqrt(jnp.mean(xf * xf, axis=-1, keepdims=True) + EPS)


def rms_norm(x, w):
    return (_rms(x) * w.astype(jnp.float32)).astype(x.dtype)


def partial_rotary(x, pos):
    half = ROT_DIM // 2
    inv_freq = jnp.asarray(np.power(ROPE_THETA, -np.arange(0, ROT_DIM, 2) / ROT_DIM).astype(np.float32))
    ang = pos.astype(jnp.float32)[:, None] * inv_freq[None, :]
    cos = jnp.cos(ang)[:, None, :]
    sin = jnp.sin(ang)[:, None, :]
    x1 = x[..., :half].astype(jnp.float32)
    x2 = x[..., half:ROT_DIM].astype(jnp.float32)
    rot = jnp.concatenate([x1 * cos - x2 * sin, x2 * cos + x1 * sin], axis=-1).astype(x.dtype)
    return jnp.concatenate([rot, x[..., ROT_DIM:]], axis=-1)


def masked_softmax(s, mask):
    p = jax.nn.softmax(jnp.where(mask, s, NEG_INF), axis=-1)
    return jnp.where(mask, p, 0.0)


def compress_blocks(k, pe, w1, w2):
    b, s, g, d = k.shape
    nc = (s - CMP_BLOCK) // CMP_STRIDE + 1
    idx = np.arange(nc)[:, None] * CMP_STRIDE + np.arange(CMP_BLOCK)[None, :]
    blk = k[:, idx] + pe[None, None, :, None, :]
    blk = jnp.swapaxes(blk, 2, 3).reshape(b, nc, g, CMP_BLOCK * d)
    return jax.nn.silu(blk @ w1) @ w2


def selection_block_map(s):
    nc = (s - CMP_BLOCK) // CMP_STRIDE + 1
    nb = s // SLC_BLOCK
    cs = np.arange(nc) * CMP_STRIDE
    bs = np.arange(nb) * SLC_BLOCK
    ov = np.minimum(cs[:, None] + CMP_BLOCK, bs[None, :] + SLC_BLOCK) - np.maximum(cs[:, None], bs[None, :])
    return jnp.asarray((np.clip(ov, 0, None) / CMP_BLOCK).astype(np.float32))


def nsa_attention(q, kc, vc, ks, vs, kw, vw, gates, pe_k, w1_k, w2_k, pe_v, w1_v, w2_v):
    b, s, h, d = q.shape
    g, j = N_KV_GROUPS, HEADS_PER_GROUP
    scale = d ** -0.5
    k_cmp = compress_blocks(kc, pe_k, w1_k, w2_k)
    v_cmp = compress_blocks(vc, pe_v, w1_v, w2_v)
    nc = k_cmp.shape[1]
    cmp_end = jnp.arange(nc) * CMP_STRIDE + CMP_BLOCK - 1
    bmap = selection_block_map(s)
    nb = s // SLC_BLOCK
    n_sel = min(SLC_TOPK, nb)
    k_blk = ks.reshape(b, nb, SLC_BLOCK, g, d).transpose(0, 3, 1, 2, 4)
    v_blk = vs.reshape(b, nb, SLC_BLOCK, g, d).transpose(0, 3, 1, 2, 4)
    kw_pad = jnp.pad(kw, ((0, 0), (WINDOW, 0), (0, 0), (0, 0)))
    vw_pad = jnp.pad(vw, ((0, 0), (WINDOW, 0), (0, 0), (0, 0)))
    b_ix = jnp.arange(b)[:, None, None, None]
    g_ix = jnp.arange(g)[None, :, None, None]
    blk_ids = jnp.arange(nb)

    def query_block(i):
        s0 = i * Q_BLOCK
        t = s0 + jnp.arange(Q_BLOCK)
        qb = lax.dynamic_slice_in_dim(q, s0, Q_BLOCK, axis=1).reshape(b, Q_BLOCK, g, j, d)
        gb = lax.dynamic_slice_in_dim(gates, s0, Q_BLOCK, axis=1).reshape(b, Q_BLOCK, g, j, 3)
        s_c = jnp.einsum('btgjd,bngd->bgjtn', qb, k_cmp, preferred_element_type=jnp.float32) * scale
        p_c = masked_softmax(s_c, cmp_end[None, :] <= t[:, None])
        o_c = jnp.einsum('bgjtn,bngd->btgjd', p_c.astype(v_cmp.dtype), v_cmp)
        imp = jnp.einsum('bgjtn,nm->bgtm', p_c, bmap)
        cur = t // SLC_BLOCK
        valid = blk_ids[None, :] <= cur[:, None]
        forced = (blk_ids[None, :] == 0) | (blk_ids[None, :] == cur[:, None]) | (blk_ids[None, :] == cur[:, None] - 1)
        imp = jnp.where(forced, jnp.inf, jnp.where(valid, imp, -jnp.inf))
        _, sel = lax.top_k(imp, n_sel)
        kg = k_blk[b_ix, g_ix, sel].reshape(b, g, Q_BLOCK, n_sel * SLC_BLOCK, d)
        vg = v_blk[b_ix, g_ix, sel].reshape(b, g, Q_BLOCK, n_sel * SLC_BLOCK, d)
        kpos = (sel[..., None] * SLC_BLOCK + jnp.arange(SLC_BLOCK)).reshape(b, g, Q_BLOCK, n_sel * SLC_BLOCK)
        m_s = (kpos <= t[None, None, :, None])[:, :, None]
        s_s = jnp.einsum('btgjd,bgtkd->bgjtk', qb, kg, preferred_element_type=jnp.float32) * scale
        p_s = masked_softmax(s_s, m_s)
        o_s = jnp.einsum('bgjtk,bgtkd->btgjd', p_s.astype(vg.dtype), vg)
        kwb = lax.dynamic_slice_in_dim(kw_pad, s0, WINDOW + Q_BLOCK, axis=1)
        vwb = lax.dynamic_slice_in_dim(vw_pad, s0, WINDOW + Q_BLOCK, axis=1)
        kpos_w = s0 - WINDOW + jnp.arange(WINDOW + Q_BLOCK)
        diff = t[:, None] - kpos_w[None, :]
        m_w = (diff >= 0) & (diff < WINDOW) & (kpos_w[None, :] >= 0)
        s_w = jnp.einsum('btgjd,bkgd->bgjtk', qb, kwb, preferred_element_type=jnp.float32) * scale
        p_w = masked_softmax(s_w, m_w)
        o_w = jnp.einsum('bgjtk,bkgd->btgjd', p_w.astype(vwb.dtype), vwb)
        o = gb[..., 0:1] * o_c + gb[..., 1:2] * o_s + gb[..., 2:3] * o_w
        return o.reshape(b, Q_BLOCK, h * d)

    out = lax.map(query_block, jnp.arange(s // Q_BLOCK))
    return jnp.swapaxes(out, 0, 1).reshape(b, s, h * d)


def ssd_chunked(x, dt, a, bm, cm):
    b, s, h, p = x.shape
    g, n = bm.shape[-2:]
    j = h // g
    q = SSM_CHUNK
    c = s // q
    x = x.astype(jnp.float32).reshape(b, c, q, g, j, p)
    dt = dt.reshape(b, c, q, g, j)
    bm = bm.astype(jnp.float32).reshape(b, c, q, g, n)
    cm = cm.astype(jnp.float32).reshape(b, c, q, g, n)
    a_cum = jnp.cumsum(jnp.moveaxis(dt * a.reshape(g, j), 2, -1), axis=-1)
    xdt = x * dt[..., None]
    causal = jnp.tril(jnp.ones((q, q), dtype=bool))
    decay_in = jnp.exp(jnp.where(causal, a_cum[..., :, None] - a_cum[..., None, :], -jnp.inf))
    cb = jnp.einsum('bclgn,bcsgn->bcgls', cm, bm)
    y_diag = jnp.einsum('bcgjls,bcsgjp->bclgjp', cb[:, :, :, None] * decay_in, xdt)
    decay_out = jnp.moveaxis(jnp.exp(a_cum[..., -1:] - a_cum), -1, 2)
    states = jnp.einsum('bcsgn,bcsgjp->bcgjpn', bm, xdt * decay_out[..., None])
    chunk_decay = jnp.exp(a_cum[..., -1])

    def step(state, inp):
        st, dec = inp
        return state * dec[..., None, None] + st, state

    h0 = jnp.zeros((b, g, j, p, n), jnp.float32)
    _, prev = lax.scan(step, h0, (jnp.moveaxis(states, 1, 0), jnp.moveaxis(chunk_decay, 1, 0)))
    prev = jnp.moveaxis(prev, 0, 1)
    y_off = jnp.einsum('bclgn,bcgjpn->bclgjp', cm, prev) * jnp.moveaxis(jnp.exp(a_cum), -1, 2)[..., None]
    return (y_diag + y_off).reshape(b, s, h, p)


def mamba2_mixer(z, xbc, dt_raw, conv_w, conv_b, dt_bias, a_log, d_skip, norm_w):
    b, s, ch = xbc.shape
    xbc = lax.conv_general_dilated(xbc, conv_w[:, None, :], window_strides=(1,), padding=((SSM_CONV - 1, 0),),
                                   dimension_numbers=('NWC', 'WIO', 'NWC'), feature_group_count=ch) + conv_b
    xbc = jax.nn.silu(xbc)
    xs, bm, cm = jnp.split(xbc, [SSM_INNER, SSM_INNER + SSM_GROUPS * SSM_STATE], axis=-1)
    xs = xs.reshape(b, s, SSM_HEADS, SSM_HEAD_DIM)
    bm = bm.reshape(b, s, SSM_GROUPS, SSM_STATE)
    cm = cm.reshape(b, s, SSM_GROUPS, SSM_STATE)
    dt = jax.nn.softplus(dt_raw.astype(jnp.float32) + dt_bias.astype(jnp.float32))
    a = -jnp.exp(a_log.astype(jnp.float32))
    y = ssd_chunked(xs, dt, a, bm, cm) + d_skip.astype(jnp.float32)[:, None] * xs.astype(jnp.float32)
    y = y.reshape(b, s, SSM_INNER) * jax.nn.silu(z.astype(jnp.float32))
    y = _rms(y.reshape(b, s, SSM_GROUPS, SSM_INNER // SSM_GROUPS)).reshape(b, s, SSM_INNER)
    return (y * norm_w.astype(jnp.float32)).astype(z.dtype)


def setup_inputs(seed: int = 0) -> dict:
    key = jax.random.key(seed)
    ks = jax.random.split(key, 26)
    L, D = DEPTH, D_MODEL

    def nrm(k, shape, sc):
        return jax.random.normal(k, shape, jnp.float32) * sc

    col_scale = np.concatenate([np.full(w, sc, np.float32) for w, sc in
                                zip(IN_WIDTHS, (1.0, 1.0, 1.0, 1.0, 1.0, DT_PROJ_SCALE, 1.0))]) * (D ** -0.5)
    dt0 = jnp.exp(jax.random.uniform(ks[14], (L, SSM_HEADS), jnp.float32, math.log(DT_MIN), math.log(DT_MAX)))
    return {
        'x': nrm(ks[0], (BATCH, SEQ, D), 1.0),
        'c': nrm(ks[1], (BATCH, D), 1.0),
        'w_ada': nrm(ks[2], (L, D, 6 * D), D ** -0.5),
        'b_ada': nrm(ks[3], (L, 6 * D), 0.02),
        'norm1_w': 1.0 + nrm(ks[4], (L, D), 0.02),
        'w_in': nrm(ks[5], (L, D, IN_TOTAL), 1.0) * jnp.asarray(col_scale),
        'cmp_pe_k': nrm(ks[6], (L, CMP_BLOCK, HEAD_DIM), 0.1),
        'cmp_w1_k': nrm(ks[7], (L, CMP_BLOCK * HEAD_DIM, CMP_HIDDEN), (CMP_BLOCK * HEAD_DIM) ** -0.5),
        'cmp_w2_k': nrm(ks[8], (L, CMP_HIDDEN, HEAD_DIM), CMP_HIDDEN ** -0.5),
        'cmp_pe_v': nrm(ks[9], (L, CMP_BLOCK, HEAD_DIM), 0.1),
        'cmp_w1_v': nrm(ks[10], (L, CMP_BLOCK * HEAD_DIM, CMP_HIDDEN), (CMP_BLOCK * HEAD_DIM) ** -0.5),
        'cmp_w2_v': nrm(ks[11], (L, CMP_HIDDEN, HEAD_DIM), CMP_HIDDEN ** -0.5),
        'conv_w': nrm(ks[12], (L, SSM_CONV, XBC_W), SSM_CONV ** -0.5),
        'conv_b': nrm(ks[13], (L, XBC_W), 0.01),
        'dt_bias': dt0 + jnp.log(-jnp.expm1(-dt0)),
        'a_log': jnp.log(jax.random.uniform(ks[15], (L, SSM_HEADS), jnp.float32, 1.0, 16.0)),
        'd_skip': 1.0 + nrm(ks[16], (L, SSM_HEADS), 0.01),
        'ssm_norm_w': 1.0 + nrm(ks[17], (L, SSM_INNER), 0.02),
        'w_attn_out': nrm(ks[18], (L, Q_W, D), Q_W ** -0.5),
        'w_ssm_out': nrm(ks[19], (L, SSM_INNER, D), SSM_INNER ** -0.5),
        'w_o': nrm(ks[20], (L, D, D), D ** -0.5),
        'norm2_w': 1.0 + nrm(ks[21], (L, D), 0.02),
        'w_gate': nrm(ks[22], (L, D, D_FF), D ** -0.5),
        'w_up': nrm(ks[23], (L, D, D_FF), D ** -0.5),
        'w_down': nrm(ks[24], (L, D_FF, D), D_FF ** -0.5),
        'norm_f_w': 1.0 + nrm(ks[25], (D,), 0.02),
    }


def reference(x, c, w_ada, b_ada, norm1_w, w_in, cmp_pe_k, cmp_w1_k, cmp_w2_k, cmp_pe_v, cmp_w1_v, cmp_w2_v,
              conv_w, conv_b, dt_bias, a_log, d_skip, ssm_norm_w, w_attn_out, w_ssm_out, w_o, norm2_w,
              w_gate, w_up, w_down, norm_f_w):
    b, s, _ = x.shape
    pos = jnp.arange(s)
    for l in range(DEPTH):
        mod = (jax.nn.silu(c) @ w_ada[l] + b_ada[l])[:, None, :]
        sh1, sc1, gt1, sh2, sc2, gt2 = jnp.split(mod, 6, axis=-1)
        h = rms_norm(x, norm1_w[l]) * (1.0 + sc1) + sh1
        proj = h @ w_in[l]
        q, kv, nsa_g, z, xbc, dt_raw, mg = jnp.split(proj, IN_OFFSETS, axis=-1)
        q = partial_rotary(q.reshape(b, s, N_HEADS, HEAD_DIM), pos)
        kv = kv.reshape(b, s, 6, N_KV_GROUPS, HEAD_DIM)
        kc = partial_rotary(kv[:, :, 0], pos)
        ks = partial_rotary(kv[:, :, 2], pos)
        kw = partial_rotary(kv[:, :, 4], pos)
        nsa_gates = jax.nn.sigmoid(nsa_g.reshape(b, s, N_HEADS, 3))
        o_attn = nsa_attention(q, kc, kv[:, :, 1], ks, kv[:, :, 3], kw, kv[:, :, 5], nsa_gates,
                               cmp_pe_k[l], cmp_w1_k[l], cmp_w2_k[l], cmp_pe_v[l], cmp_w1_v[l], cmp_w2_v[l])
        y_attn = o_attn @ w_attn_out[l]
        o_ssm = mamba2_mixer(z, xbc, dt_raw, conv_w[l], conv_b[l], dt_bias[l], a_log[l], d_skip[l], ssm_norm_w[l])
        y_ssm = o_ssm @ w_ssm_out[l]
        g_attn, g_ssm = jnp.split(jax.nn.sigmoid(mg), 2, axis=-1)
        x = x + gt1 * ((g_attn * y_attn + g_ssm * y_ssm) @ w_o[l])
        h = rms_norm(x, norm2_w[l]) * (1.0 + sc2) + sh2
        x = x + gt2 * ((jax.nn.silu(h @ w_gate[l]) * (h @ w_up[l])) @ w_down[l])
    return rms_norm(x, norm_f_w)
```

```python
from contextlib import ExitStack
import numpy as np
import ml_dtypes
import concourse.bass as bass
import concourse.mybir as mybir
from concourse.bass_utils import run_bass_kernel_spmd

F32 = mybir.dt.float32
BF16 = mybir.dt.bfloat16
AF = mybir.ActivationFunctionType
ALU = mybir.AluOpType
AX = mybir.AxisListType
NPBF = ml_dtypes.bfloat16

S_LEN = 4096
D = 1024
NT = 8
NST = 32
IN_TOTAL = 9040
OFF_Q, OFF_KV, OFF_G, OFF_Z, OFF_XBC, OFF_DT, OFF_MG = 0, 1024, 1792, 1840, 3888, 6960, 6992
DFF = 2816
NEG = -30000.0


class Buf:
    __slots__ = ("name", "w", "r")

    def __init__(self, name=""):
        self.name = name
        self.w = None
        self.r = []


class _Rec:
    def __getattr__(self, name):
        return lambda *a, **k: (name, a, k)


_REC = _Rec()


class Sched:
    ENGS = ("pe", "act", "dve", "pool", "sp")

    def __init__(self, nc, stack, n_dma_sems=40):
        self.nc = nc
        self.prog = {e: [] for e in self.ENGS}
        self.sem = {}
        for e in self.ENGS:
            self.sem[e] = stack.enter_context(nc.semaphore("tl_" + e))
        self.cnt = {e: 0 for e in self.ENGS}
        self.known = {e: {} for e in self.ENGS}
        self.dsems = []
        for i in range(n_dma_sems):
            s = stack.enter_context(nc.semaphore("dq%d" % i))
            self.sem["dq%d" % i] = s
            self.dsems.append(["dq%d" % i, 0])
        self.dnext = 0
        self.ninst = 0

    def _wait(self, eng, key, val):
        if val <= 0:
            return
        k = self.known[eng]
        if k.get(key, 0) >= val:
            return
        k[key] = val
        self.prog[eng].append(("w", key, val))

    def _deps(self, eng, reads, writes, skip_self=False):
        for b in reads:
            if b.w is not None and not (skip_self and b.w[0] == eng):
                self._wait(eng, *b.w)
        for b in writes:
            if b.w is not None and not (skip_self and b.w[0] == eng):
                self._wait(eng, *b.w)
            for rr in b.r:
                if not (skip_self and rr[0] == eng):
                    self._wait(eng, *rr)

    def _mark(self, ev, reads, writes):
        for b in reads:
            b.r.append(ev)
            if len(b.r) > 48:
                d = {}
                for k, v in b.r:
                    d[k] = max(d.get(k, 0), v)
                b.r = list(d.items())
        for b in writes:
            b.w = ev
            b.r = []

    def op(self, eng, fn, reads=(), writes=(), skip_self=False):
        self._deps(eng, reads, writes, skip_self)
        self.cnt[eng] += 1
        ev = (eng, self.cnt[eng])
        self.prog[eng].append(("i", fn(_REC)))
        self._mark(ev, reads, writes)
        self.ninst += 1
        return ev

    def dma(self, q, fn, reads=(), writes=()):
        slot = self.dsems[self.dnext]
        self.dnext = (self.dnext + 1) % len(self.dsems)
        key = slot[0]
        self._wait(q, key, slot[1])
        self._deps(q, reads, writes)
        slot[1] += 16
        ev = (key, slot[1])
        self.prog[q].append(("d", fn(_REC), key))
        self._mark(ev, reads, writes)
        self.ninst += 1
        return ev

    def barrier(self):
        for e in self.ENGS:
            for e2 in self.ENGS:
                if e2 != e:
                    self._wait(e, e2, self.cnt[e2])
            for key, val in self.dsems:
                self._wait(e, key, val)

    def wait_all(self, eng, evs):
        for ev in evs:
            self._wait(eng, *ev)

    def emit(self):
        nc, sem, prog = self.nc, self.sem, self.prog
        with nc.Block() as block:
            def run(engname):
                def body(e):
                    tl = sem[engname]
                    for it in prog[engname]:
                        if it[0] == "w":
                            e.wait_ge(sem[it[1]], it[2])
                        elif it[0] == "i":
                            nm, a, k = it[1]
                            getattr(e, nm)(*a, **k).then_inc(tl, 1)
                        else:
                            nm, a, k = it[1]
                            getattr(e, nm)(*a, **k).then_inc(sem[it[2]], 16)
                return body
            block.tensor(run("pe"))
            block.scalar(run("act"))
            block.vector(run("dve"))
            block.gpsimd(run("pool"))
            block.sync(run("sp"))


def pipeline(n, stages):
    K = len(stages)
    for step in range(n + K - 1):
        for k in reversed(range(K)):
            i = step - k
            if 0 <= i < n:
                stages[k](i)


def host_consts():
    c = {}
    pos = np.arange(S_LEN, dtype=np.float32)
    inv_freq = np.power(500000.0, -np.arange(0, 16, 2) / 16).astype(np.float32)
    ang = (pos[:, None] * inv_freq[None, :]).astype(np.float32)
    cs, sn = np.cos(ang).astype(np.float32), np.sin(ang).astype(np.float32)
    C = np.ones((128, S_LEN), np.float32)
    Sm = np.zeros((128, S_LEN), np.float32)
    for p in range(128):
        dd = p % 64
        if dd < 16:
            C[p] = cs[:, dd % 8]
            Sm[p] = sn[:, dd % 8]
    c["ropeC"] = C.astype(NPBF)
    c["ropeS"] = Sm.astype(NPBF)
    P = np.zeros((128, 128), np.float32)
    for hb in (0, 64):
        for dd in range(8):
            P[hb + dd + 8, hb + dd] = -1.0
            P[hb + dd, hb + dd + 8] = 1.0
    c["pmat"] = P.astype(NPBF)
    c["ident"] = np.eye(128, dtype=np.float32).astype(NPBF)
    c["identf"] = np.eye(128, dtype=np.float32)
    s = np.arange(128)
    c["utri"] = (s[:, None] <= s[None, :]).astype(np.float32)
    c["negtri"] = np.where(s[:, None] <= s[None, :], 0.0, NEG).astype(np.float32)
    sb = np.zeros((128, NST, 64), np.float32)
    for st in range(NST):
        t = st * 128 + s
        cur = t // 64
        m = np.arange(64)
        forced = (m[None, :] == 0) | (m[None, :] == cur[:, None]) | (m[None, :] == cur[:, None] - 1)
        valid = m[None, :] <= cur[:, None]
        sb[:, st, :] = np.where(forced, 64.0, np.where(valid, 0.0, -64.0))
    c["selbias"] = sb
    ncmp = 255
    cst = np.arange(ncmp) * 16
    bst = np.arange(64) * 64
    ov = np.minimum(cst[:, None] + 32, bst[None, :] + 64) - np.maximum(cst[:, None], bst[None, :])
    bmap = np.zeros((256, 64), np.float32)
    bmap[:255] = np.clip(ov, 0, None) / 32.0
    c["bmap"] = np.ascontiguousarray(bmap.reshape(2, 128, 64).transpose(1, 0, 2)).astype(NPBF)
    E = np.zeros((128, S_LEN), np.float32)
    E[np.arange(S_LEN) // 64, np.arange(S_LEN)] = 1.0
    E[64 + np.arange(S_LEN) // 64, np.arange(S_LEN)] = 1.0
    c["emat"] = E.astype(NPBF)
    return c


CONST_SPECS = {
    "ropeC": ([128, S_LEN], BF16), "ropeS": ([128, S_LEN], BF16), "pmat": ([128, 128], BF16),
    "ident": ([128, 128], BF16), "identf": ([128, 128], F32), "utri": ([128, 128], F32),
    "negtri": ([128, 128], F32), "selbias": ([128, NST, 64], F32), "bmap": ([128, 2, 64], BF16),
    "emat": ([128, S_LEN], BF16),
}

IN_SPECS = {
    "x": [S_LEN, D], "c_col": [128, 8], "w_ada": [D, 6 * D], "b_ada_col": [128, 48], "b_ada_row": [1, 6 * D],
    "norm1_col": [128, 8], "w_in": [D, IN_TOTAL],
    "cmp_pe_k": [32, 64], "cmp_w1_k": [2048, 256], "cmp_w2_k": [256, 64],
    "cmp_pe_v": [32, 64], "cmp_w1_v": [2048, 256], "cmp_w2_v": [256, 64],
    "conv_w_col": [128, 24, 4], "conv_b_col": [128, 24], "dt_bias": [1, 32], "a_log": [1, 32], "d_skip": [1, 32],
    "ssm_norm_w": [1, 2048], "w_attn_out": [D, D], "w_ssm_out": [2048, D], "w_o": [D, D],
    "norm2_col": [128, 8], "w_gate": [D, DFF], "w_up": [D, DFF], "w_down": [DFF, D], "norm_f_w": [1, D],
}


def build(dbg=(), stop_after=99, p4_only=None, skip=(), p5_cut=99, p5_list=None, p5_nostate=False):
    nc = bass.Bass("TRN2", target_bir_lowering=False)
    I = {k: nc.dram_tensor(k, sh, F32, kind="ExternalInput").ap() for k, sh in IN_SPECS.items()}
    CN = {k: nc.dram_tensor(k, sh, dt, kind="ExternalInput").ap() for k, (sh, dt) in CONST_SPECS.items()}
    out_d = nc.dram_tensor("out", [S_LEN, D], F32, kind="ExternalOutput").ap()

    def scratch(name, shape, dt):
        kind = "ExternalOutput" if name in dbg else "Internal"
        return nc.dram_tensor(name, shape, dt, kind=kind).ap()

    qT_d = scratch("qT_d", [8, 128, S_LEN], BF16)
    z_d = scratch("z_d", [S_LEN, 2048], BF16)
    xtok_d = scratch("xtok_d", [S_LEN, 2048], BF16)
    bmtok_d = scratch("bmtok_d", [S_LEN, 512], BF16)
    bcT_d = scratch("bcT_d", [8, 128, S_LEN], BF16)
    gT_d = scratch("gT_d", [16, 128, S_LEN], BF16)
    yaT_d = scratch("yaT_d", [8, 128, S_LEN], BF16)
    ysT_d = scratch("ysT_d", [8, 128, S_LEN], BF16)
    x1_d = scratch("x1_d", [S_LEN, D], F32)
    actT_d = scratch("actT_d", [22, 128, S_LEN], BF16)
    dbg_outs = {}

    with ExitStack() as st:
        S = Sched(nc, st)

        def T(name, shape, dt, stack=st):
            return stack.enter_context(nc.sbuf_tensor("s_" + name, shape, dt))

        def PS(name, shape, dt=F32, stack=st):
            return stack.enter_context(nc.psum_tensor("p_" + name, shape, dt))

        final_evs = []

        def dbg_dump(name, tile_ap, shape, dt, bufs):
            if name in dbg:
                d_ = nc.dram_tensor(name, shape, dt, kind="ExternalOutput").ap()
                final_evs.append(S.dma("sp", lambda e: e.dma_start(out=d_, in_=tile_ap), reads=bufs))

        ident = T("ident", [128, 128], BF16); b_ident = Buf()
        identf = T("identf", [128, 128], F32); b_identf = Buf()
        S.dma("sp", lambda e: e.dma_start(out=ident[:], in_=CN["ident"]), writes=[b_ident])
        S.dma("sp", lambda e: e.dma_start(out=identf[:], in_=CN["identf"]), writes=[b_identf])
        eps_t = T("eps_t", [128, 1], F32); b_eps = Buf()
        S.op("pool", lambda e: e.memset(eps_t[:], 1e-6), writes=[b_eps])

        banks = [PS("bank%d" % i, [128, 512], F32) for i in range(8)]
        b_bank = [Buf("bank%d" % i) for i in range(8)]

        modT = T("modT", [128, 48], F32); b_modT = Buf()
        A1 = T("A1", [128, 8], F32); b_A1 = Buf()
        A2 = T("A2", [128, 8], F32); b_A2 = Buf()
        gtb_d = scratch("gtb_d", [2, 128, D], F32); b_gtb_d = Buf()
        with ExitStack() as p0:
            gt1_b = T("gt1_b", [128, D], F32, p0); b_gt1 = Buf()
            gt2_b = T("gt2_b", [128, D], F32, p0); b_gt2 = Buf()
            ccol = T("ccol", [128, 8], F32, p0); b_ccol = Buf()
            scol = T("scol", [128, 8], F32, p0); b_scol = Buf()
            bcol = T("bcol", [128, 48], F32, p0); b_bcol = Buf()
            n1c = T("n1c", [128, 8], F32, p0); b_n1c = Buf()
            n2c = T("n2c", [128, 8], F32, p0); b_n2c = Buf()
            brow = T("brow", [128, 2, D], F32, p0); b_brow = Buf()
            wa = [T("wa%d" % i, [128, 8, 512], F32, p0) for i in range(3)]
            b_wa = [Buf() for _ in range(3)]
            S.dma("sp", lambda e: e.dma_start(out=ccol[:], in_=I["c_col"]), writes=[b_ccol])
            S.dma("sp", lambda e: e.dma_start(out=bcol[:], in_=I["b_ada_col"]), writes=[b_bcol])
            S.dma("sp", lambda e: e.dma_start(out=n1c[:], in_=I["norm1_col"]), writes=[b_n1c])
            S.dma("sp", lambda e: e.dma_start(out=n2c[:], in_=I["norm2_col"]), writes=[b_n2c])
            S.dma("sp", lambda e: e.dma_start(out=brow[:, 0, :], in_=I["b_ada_row"][:, 2 * D:3 * D].partition_broadcast(128)), writes=[b_brow])
            S.dma("sp", lambda e: e.dma_start(out=brow[:, 1, :], in_=I["b_ada_row"][:, 5 * D:6 * D].partition_broadcast(128)), writes=[b_brow])
            S.op("act", lambda e: e.activation(out=scol[:], in_=ccol[:], func=AF.Silu), reads=[b_ccol], writes=[b_scol])
            wv = I["w_ada"].rearrange("(kc p) n -> p kc n", p=128)
            mod_ps = banks[0]
            for blk in range(12):
                wt, bw = wa[blk % 3], b_wa[blk % 3]
                S.dma("sp", lambda e, wt=wt, blk=blk: e.dma_start(out=wt[:], in_=wv[:, :, blk * 512:(blk + 1) * 512]), writes=[bw])
                j = blk // 2
                if j in (2, 5):
                    bk = banks[1 + (blk % 2)]; bb = b_bank[1 + (blk % 2)]
                    for kc in range(8):
                        S.op("pe", lambda e, wt=wt, kc=kc, bk=bk: e.matmul(bk[:], lhsT=scol[:, kc:kc + 1].to_broadcast([128, 128]), rhs=wt[:, kc, :], start=(kc == 0), stop=(kc == 7)),
                             reads=[b_scol, bw], writes=[bb], skip_self=True)
                    dst = gt1_b if j == 2 else gt2_b
                    bd = b_gt1 if j == 2 else b_gt2
                    half = blk % 2
                    S.op("dve", lambda e, dst=dst, bk=bk, half=half, j=j: e.tensor_tensor(out=dst[:, half * 512:(half + 1) * 512], in0=bk[:], in1=brow[:, 0 if j == 2 else 1, half * 512:(half + 1) * 512], op=ALU.add),
                         reads=[bb, b_brow], writes=[bd])
                else:
                    for fc in range(4):
                        col = blk * 4 + fc
                        for kc in range(8):
                            S.op("pe", lambda e, wt=wt, kc=kc, fc=fc, col=col: e.matmul(mod_ps[:, col:col + 1], lhsT=wt[:, kc, fc * 128:(fc + 1) * 128], rhs=scol[:, kc:kc + 1], start=(kc == 0), stop=(kc == 7)),
                                 reads=[b_scol, bw], writes=[b_bank[0]], skip_self=True)
            S.op("dve", lambda e: e.tensor_tensor(out=modT[:, 0:16], in0=mod_ps[:, 0:16], in1=bcol[:, 0:16], op=ALU.add), reads=[b_bank[0], b_bcol], writes=[b_modT])
            S.op("dve", lambda e: e.tensor_tensor(out=modT[:, 24:40], in0=mod_ps[:, 24:40], in1=bcol[:, 24:40], op=ALU.add), reads=[b_bank[0], b_bcol], writes=[b_modT])
            S.op("dve", lambda e: e.scalar_tensor_tensor(out=A1[:], in0=modT[:, 8:16], scalar=1.0, in1=n1c[:], op0=ALU.add, op1=ALU.mult), reads=[b_modT, b_n1c], writes=[b_A1])
            S.op("dve", lambda e: e.scalar_tensor_tensor(out=A2[:], in0=modT[:, 32:40], scalar=1.0, in1=n2c[:], op0=ALU.add, op1=ALU.mult), reads=[b_modT, b_n2c], writes=[b_A2])
            dbg_dump("dbg_modT", modT[:, 0:16], [128, 16], F32, [b_modT])
            dbg_dump("dbg_gt1", gt1_b[:], [128, D], F32, [b_gt1])
            S.dma("sp", lambda e: e.dma_start(out=gtb_d[0], in_=gt1_b[:]), reads=[b_gt1], writes=[b_gtb_d])
            S.dma("sp", lambda e: e.dma_start(out=gtb_d[1], in_=gt2_b[:]), reads=[b_gt2], writes=[b_gtb_d])
            S.barrier()
        sh1c = modT[:, 0:8]
        sh2c = modT[:, 24:32]

        if stop_after <= 0:
            S.wait_all("sp", final_evs)
            S.emit()
            return nc

        dt_sb = T("dt_sb", [128, NST, 32], F32); b_dt = [Buf() for _ in range(NST)]
        pk = st.enter_context(ExitStack())
        ksT = [T("ksT%d" % g, [128, S_LEN], BF16, pk) for g in range(2)]; b_ksT = [[Buf() for _ in range(NT)] for _ in range(2)]
        kwT = [T("kwT%d" % g, [128, S_LEN], BF16, pk) for g in range(2)]; b_kwT = [[Buf() for _ in range(NT)] for _ in range(2)]
        vs_aug = T("vs_aug", [128, NST, 2, 65], BF16, pk); b_vs = [Buf() for _ in range(NST)]
        vw_aug = T("vw_aug", [128, NST, 2, 65], BF16, pk); b_vw = [Buf() for _ in range(NST)]
        gates_sb = T("gates_sb", [128, NST, 48], F32, pk); b_gates = [Buf() for _ in range(NST)]
        S.op("pool", lambda e: e.memset(vs_aug[:, :, :, 64:65], 1.0), writes=b_vs)
        S.op("pool", lambda e: e.memset(vw_aug[:, :, :, 64:65], 1.0), writes=b_vw)
        kcvc_d = scratch("kcvc_d", [2, 128, S_LEN], BF16); b_kcvc_d = [Buf(), Buf()]
        b_qT_d = [[Buf() for _ in range(NT)] for _ in range(8)]
        b_z_d = [Buf() for _ in range(NT)]
        b_gT_d = [Buf() for _ in range(NT)]
        b_xtok_d = [Buf() for _ in range(NT)]; b_bmtok_d = [Buf() for _ in range(NT)]; b_bcT_d = [Buf() for _ in range(NT)]
        b_yaT_d = [Buf() for _ in range(NT)]; b_ysT_d = [Buf() for _ in range(NT)]
        ph = st.enter_context(ExitStack())
        hT = T("hT", [128, 8, S_LEN], BF16, ph)
        b_hT = [Buf("hT%d" % i) for i in range(NST)]
        with ExitStack() as p1:
            xt = [T("p1x%d" % i, [128, D], F32, p1) for i in range(6)]; b_xt = [Buf() for _ in range(6)]
            junk = T("p1junk", [128, D], BF16, p1); b_junk = Buf()
            ssq = T("p1ss", [128, NST], F32, p1); b_ss = [Buf() for _ in range(NST)]
            xn = [T("p1xn%d" % i, [128, D], BF16, p1) for i in range(2)]; b_xn = [Buf() for _ in range(2)]
            tp = [banks[0].bitcast(BF16), banks[1].bitcast(BF16)]

            def s_load(i):
                S.dma("sp", lambda e: e.dma_start(out=xt[i % 6][:], in_=I["x"][i * 128:(i + 1) * 128, :]), writes=[b_xt[i % 6]])

            def s_sq(i):
                S.op("act", lambda e: e.activation(out=junk[:], in_=xt[i % 6][:], func=AF.Square, accum_out=ssq[:, i:i + 1]),
                     reads=[b_xt[i % 6]], writes=[b_junk, b_ss[i]])

            def s_rs1(i):
                S.op("dve", lambda e: e.tensor_scalar(out=ssq[:, i:i + 1], in0=ssq[:, i:i + 1], scalar1=1.0 / D, scalar2=1e-6, op0=ALU.mult, op1=ALU.add),
                     reads=[b_ss[i]], writes=[b_ss[i]])

            def s_rs2(i):
                S.op("act", lambda e: e.activation(out=ssq[:, i:i + 1], in_=ssq[:, i:i + 1], func=AF.Sqrt), reads=[b_ss[i]], writes=[b_ss[i]])

            def s_rs3(i):
                S.op("dve", lambda e: e.reciprocal(out=ssq[:, i:i + 1], in_=ssq[:, i:i + 1]), reads=[b_ss[i]], writes=[b_ss[i]])
                S.op("dve", lambda e: e.tensor_scalar(out=xn[i % 2][:], in0=xt[i % 6][:], scalar1=ssq[:, i:i + 1], scalar2=None, op0=ALU.mult),
                     reads=[b_xt[i % 6], b_ss[i]], writes=[b_xn[i % 2]])

            def s_tr(i):
                pb = tp[i % 2]; bb = b_bank[i % 2]
                for c in range(8):
                    S.op("pe", lambda e, c=c: e.transpose(out=pb[:, c * 128:(c + 1) * 128], in_=xn[i % 2][:, c * 128:(c + 1) * 128], identity=ident[:]),
                         reads=[b_xn[i % 2], b_ident], writes=[bb], skip_self=True)

            def s_ev(i):
                pb = tp[i % 2]; bb = b_bank[i % 2]
                for c in range(8):
                    S.op("act", lambda e, c=c: e.activation(out=hT[:, c, i * 128:(i + 1) * 128], in_=pb[:, c * 128:(c + 1) * 128], func=AF.Identity,
                                                             scale=A1[:, c:c + 1], bias=sh1c[:, c:c + 1]),
                         reads=[bb, b_A1, b_modT], writes=[b_hT[i]])

            pipeline(NST, [s_load, s_sq, s_rs1, s_rs2, s_rs3, s_tr, s_ev])
            if "dbg_hT" in dbg:
                dbg_dump("dbg_hT", hT[:, :, 0:512], [128, 8, 512], BF16, b_hT[0:4])
            S.barrier()

        if stop_after <= 1:
            S.wait_all("sp", final_evs)
            S.emit()
            return nc

        def make_wloader(stack, w_view, nw=2, nstg=2, tag="w"):
            wt = [T("%swt%d" % (tag, i), [128, 8, 512], BF16, stack) for i in range(nw)]; b_wt = [Buf() for _ in range(nw)]
            fs = [T("%sfs%d" % (tag, i), [128, 8, 256], F32, stack) for i in range(nstg)]; b_fs = [Buf() for _ in range(nstg)]
            ctr = [0, 0]

            def load(pieces):
                i = ctr[0] % nw; ctr[0] += 1
                for pc_ in pieces:
                    (c0, n, dsts) = pc_[:3]
                    wv_ = pc_[3] if len(pc_) > 3 else w_view
                    for o in range(0, n, 256):
                        m = min(256, n - o)
                        j = ctr[1] % nstg; ctr[1] += 1
                        S.dma("sp", lambda e, j=j, c0=c0, o=o, m=m, wv_=wv_: e.dma_start(out=fs[j][:, :, 0:m], in_=wv_[:, :, c0 + o:c0 + o + m]), writes=[b_fs[j]])
                        for d0 in dsts:
                            S.op("pool", lambda e, i=i, j=j, d0=d0, o=o, m=m: e.tensor_copy(out=wt[i][:, :, d0 + o:d0 + o + m], in_=fs[j][:, :, 0:m]), reads=[b_fs[j]], writes=[b_wt[i]])
                return wt[i], b_wt[i]
            return load

        w_in_v = I["w_in"].rearrange("(kc p) n -> p kc n", p=128)

        with ExitStack() as p2:
            ropeC = T("ropeC", [128, S_LEN], BF16, p2); b_ropeC = Buf()
            ropeS = T("ropeS", [128, S_LEN], BF16, p2); b_ropeS = Buf()
            pmat = T("pmat", [128, 128], BF16, p2); b_pmat = Buf()
            S.dma("sp", lambda e: e.dma_start(out=ropeC[:], in_=CN["ropeC"]), writes=[b_ropeC])
            S.dma("sp", lambda e: e.dma_start(out=ropeS[:], in_=CN["ropeS"]), writes=[b_ropeS])
            S.dma("sp", lambda e: e.dma_start(out=pmat[:], in_=CN["pmat"]), writes=[b_pmat])
            cwc = T("cwc", [128, 24, 4], F32, p2); b_cwc = Buf()
            cbc = T("cbc", [128, 24], F32, p2); b_cbc = Buf()
            dtb = T("dtb", [128, 32], F32, p2); b_dtb = Buf()
            S.dma("sp", lambda e: e.dma_start(out=cwc[:], in_=I["conv_w_col"]), writes=[b_cwc])
            S.dma("sp", lambda e: e.dma_start(out=cbc[:], in_=I["conv_b_col"]), writes=[b_cbc])
            S.dma("sp", lambda e: e.dma_start(out=dtb[:], in_=I["dt_bias"].partition_broadcast(128)), writes=[b_dtb])
            load_w = make_wloader(p2, w_in_v, tag="p2")

            acc_banks = [0, 1]; rot_banks = [2, 3]; tr_banks = [4, 5]; cv_banks = [6, 7]
            actr = [0]
            qraw = [T("p2qraw%d" % i, [128, 512], BF16, p2) for i in range(2)]; b_qraw = [Buf() for _ in range(2)]
            t1 = [T("p2t1_%d" % i, [128, 512], F32, p2) for i in range(2)]; b_t1 = [Buf() for _ in range(2)]
            t2 = [T("p2t2_%d" % i, [128, 512], F32, p2) for i in range(2)]; b_t2 = [Buf() for _ in range(2)]
            stg = [T("p2stg%d" % i, [128, 512], BF16, p2) for i in range(3)]; b_stg = [Buf() for _ in range(3)]
            sctr = [0]
            zst = [T("p2zst%d" % i, [128, 2, 512], BF16, p2) for i in range(2)]; b_zst = [Buf() for _ in range(2)]
            pre = [T("p2pre%d" % i, [128, 3 + 512], BF16, p2) for i in range(2)]; b_pre = [Buf() for _ in range(2)]
            dg = [T("p2dg%d" % i, [128, 4, 128], BF16, p2) for i in range(2)]; b_dg = [Buf() for _ in range(2)]
            xact = [T("p2xact%d" % i, [128, 512], BF16, p2) for i in range(3)]; b_xact = [Buf() for _ in range(3)]
            xst = [T("p2xst%d" % i, [128, 4, 128], BF16, p2) for i in range(3)]; b_xst = [Buf() for _ in range(3)]
            xctr = [0]

            def fm_matmul(w, bw, wcol0, M, tt):
                k = acc_banks[actr[0] % 2]; actr[0] += 1
                for kc in range(8):
                    S.op("pe", lambda e, kc=kc, k=k: e.matmul(banks[k][0:M, :], lhsT=w[:, kc, wcol0:wcol0 + M], rhs=hT[:, kc, tt * 512:(tt + 1) * 512], start=(kc == 0), stop=(kc == 7)),
                         reads=[bw] + b_hT[tt * 4:tt * 4 + 4], writes=[b_bank[k]], skip_self=True)
                return k

            def rotary_epi(k, tt, dst_ap, dst_bufs):
                i = actr[0] % 2
                rk = rot_banks[i]
                S.op("act", lambda e: e.activation(out=qraw[i][:], in_=banks[k][:], func=AF.Copy), reads=[b_bank[k]], writes=[b_qraw[i]])
                S.op("pe", lambda e: e.matmul(banks[rk][:], lhsT=pmat[:], rhs=qraw[i][:], start=True, stop=True), reads=[b_pmat, b_qraw[i]], writes=[b_bank[rk]])
                S.op("pool", lambda e: e.tensor_tensor(out=t1[i][:], in0=qraw[i][:], in1=ropeC[:, tt * 512:(tt + 1) * 512], op=ALU.mult), reads=[b_qraw[i], b_ropeC], writes=[b_t1[i]])
                S.op("dve", lambda e: e.tensor_tensor(out=t2[i][:], in0=banks[rk][:], in1=ropeS[:, tt * 512:(tt + 1) * 512], op=ALU.mult), reads=[b_bank[rk], b_ropeS], writes=[b_t2[i]])
                S.op("dve", lambda e: e.tensor_tensor(out=dst_ap, in0=t1[i][:], in1=t2[i][:], op=ALU.add), reads=[b_t1[i], b_t2[i]], writes=dst_bufs)

            def spill_stage():
                si = sctr[0] % 3; sctr[0] += 1
                return si

            def g_q(cg):
                def f(w, bw):
                    for cc in range(4):
                        c = cg * 4 + cc
                        for tt in range(NT):
                            k = fm_matmul(w, bw, cc * 128, 128, tt)
                            si = spill_stage()
                            rotary_epi(k, tt, stg[si][:], [b_stg[si]])
                            S.dma("sp", lambda e, si=si, c=c, tt=tt: e.dma_start(out=qT_d[c, :, tt * 512:(tt + 1) * 512], in_=stg[si][:]), reads=[b_stg[si]], writes=[b_qT_d[c][tt]])
                return f

            def g_kv1(w, bw):
                for tt in range(NT):
                    k = fm_matmul(w, bw, 0, 128, tt)
                    si = spill_stage()
                    rotary_epi(k, tt, stg[si][:], [b_stg[si]])
                    S.dma("sp", lambda e, si=si, tt=tt: e.dma_start(out=kcvc_d[0, :, tt * 512:(tt + 1) * 512], in_=stg[si][:]), reads=[b_stg[si]], writes=[b_kcvc_d[0]])
                for tt in range(NT):
                    k = fm_matmul(w, bw, 128, 128, tt)
                    si = spill_stage()
                    S.op("act", lambda e, k=k, si=si: e.activation(out=stg[si][:], in_=banks[k][:], func=AF.Copy), reads=[b_bank[k]], writes=[b_stg[si]])
                    S.dma("sp", lambda e, si=si, tt=tt: e.dma_start(out=kcvc_d[1, :, tt * 512:(tt + 1) * 512], in_=stg[si][:]), reads=[b_stg[si]], writes=[b_kcvc_d[1]])
                for g in range(2):
                    for tt in range(NT):
                        k = fm_matmul(w, bw, 256 + g * 128, 128, tt)
                        rotary_epi(k, tt, ksT[g][:, tt * 512:(tt + 1) * 512], [b_ksT[g][tt]])

            def g_kv2(w, bw):
                for g in range(2):
                    for tt in range(NT):
                        k = fm_matmul(w, bw, g * 128, 128, tt)
                        rotary_epi(k, tt, kwT[g][:, tt * 512:(tt + 1) * 512], [b_kwT[g][tt]])
                for sti in range(NST):
                    k = acc_banks[actr[0] % 2]; actr[0] += 1
                    for kc in range(8):
                        S.op("pe", lambda e, kc=kc, k=k, sti=sti: e.matmul(banks[k][:, 0:256], lhsT=hT[:, kc, sti * 128:(sti + 1) * 128], rhs=w[:, kc, 256:512], start=(kc == 0), stop=(kc == 7)),
                             reads=[bw, b_hT[sti]], writes=[b_bank[k]], skip_self=True)
                    S.op("act", lambda e, k=k, sti=sti: e.activation(out=vs_aug[:, sti, :, 0:64], in_=banks[k][:, 0:128].rearrange("p (g d) -> p g d", g=2), func=AF.Copy), reads=[b_bank[k]], writes=[b_vs[sti]])
                    S.op("act", lambda e, k=k, sti=sti: e.activation(out=vw_aug[:, sti, :, 0:64], in_=banks[k][:, 128:256].rearrange("p (g d) -> p g d", g=2), func=AF.Copy), reads=[b_bank[k]], writes=[b_vw[sti]])

            def g_small(w, bw):
                for sti in range(NST):
                    k = acc_banks[actr[0] % 2]; actr[0] += 1
                    for kc in range(8):
                        S.op("pe", lambda e, kc=kc, k=k, sti=sti: e.matmul(banks[k][:, 0:80], lhsT=hT[:, kc, sti * 128:(sti + 1) * 128], rhs=w[:, kc, 0:80], start=(kc == 0), stop=(kc == 7)),
                             reads=[bw, b_hT[sti]], writes=[b_bank[k]], skip_self=True)
                    S.op("dve", lambda e, k=k, sti=sti: e.tensor_copy(out=gates_sb[:, sti, :], in_=banks[k][:, 0:48]), reads=[b_bank[k]], writes=[b_gates[sti]])
                    S.op("dve", lambda e, k=k, sti=sti: e.tensor_tensor(out=dt_sb[:, sti, :], in0=banks[k][:, 48:80], in1=dtb[:], op=ALU.add), reads=[b_bank[k], b_dtb], writes=[b_dt[sti]])
                S.op("act", lambda e: e.activation(out=gates_sb[:], in_=gates_sb[:], func=AF.Sigmoid), reads=b_gates, writes=b_gates)
                S.op("act", lambda e: e.activation(out=dt_sb[:], in_=dt_sb[:], func=AF.Exp), reads=b_dt, writes=b_dt)
                S.op("act", lambda e: e.activation(out=dt_sb[:], in_=dt_sb[:], func=AF.Ln, bias=1.0), reads=b_dt, writes=b_dt)

            def g_z(zb):
                def f(w, bw):
                    for tt in range(2 * NT):
                        zi = tt % 2
                        for j in range(2):
                            sti = tt * 2 + j
                            k = acc_banks[actr[0] % 2]; actr[0] += 1
                            for kc in range(8):
                                S.op("pe", lambda e, kc=kc, k=k, sti=sti: e.matmul(banks[k][:], lhsT=hT[:, kc, sti * 128:(sti + 1) * 128], rhs=w[:, kc, :], start=(kc == 0), stop=(kc == 7)),
                                     reads=[bw, b_hT[sti]], writes=[b_bank[k]], skip_self=True)
                            S.op("act", lambda e, k=k, zi=zi, j=j: e.activation(out=zst[zi][:, j, :], in_=banks[k][:], func=AF.Silu), reads=[b_bank[k]], writes=[b_zst[zi]])
                        S.dma("sp", lambda e, zi=zi, tt=tt: e.dma_start(out=z_d[tt * 256:(tt + 1) * 256, zb * 512:(zb + 1) * 512].rearrange("(j p) n -> p j n", p=128), in_=zst[zi][:]),
                              reads=[b_zst[zi]], writes=[b_z_d[tt // 2]])
                return f

            def g_mg(cg):
                def f(w, bw):
                    for cc in range(4):
                        c = cg * 4 + cc
                        for tt in range(NT):
                            k = fm_matmul(w, bw, cc * 128, 128, tt)
                            si = spill_stage()
                            S.op("act", lambda e, k=k, si=si: e.activation(out=stg[si][:], in_=banks[k][:], func=AF.Sigmoid), reads=[b_bank[k]], writes=[b_stg[si]])
                            S.dma("sp", lambda e, si=si, c=c, tt=tt: e.dma_start(out=gT_d[c, :, tt * 512:(tt + 1) * 512], in_=stg[si][:]), reads=[b_stg[si]], writes=[b_gT_d[tt]])
                return f

            def g_xbc(cg):
                def f(w, bw):
                    for cc in range(4):
                        ch = cg * 4 + cc
                        di = ch % 2
                        for i4 in range(4):
                            S.op("dve", lambda e, di=di, i4=i4, ch=ch: e.tensor_scalar(out=dg[di][:, i4, :], in0=ident[:], scalar1=cwc[:, ch, i4:i4 + 1], scalar2=None, op0=ALU.mult),
                                 reads=[b_ident, b_cwc], writes=[b_dg[di]])
                        for tt in range(NT):
                            k = fm_matmul(w, bw, cc * 128, 128, tt)
                            pi = xctr[0] % 2; xi = xctr[0] % 3; xctr[0] += 1
                            if tt == 0:
                                S.op("pool", lambda e, pi=pi: e.memset(pre[pi][:, 0:3], 0.0), writes=[b_pre[pi]])
                            else:
                                S.op("pool", lambda e, pi=pi: e.tensor_copy(out=pre[pi][:, 0:3], in_=pre[1 - pi][:, 512:515]), reads=[b_pre[1 - pi]], writes=[b_pre[pi]])
                            S.op("act", lambda e, k=k, pi=pi: e.activation(out=pre[pi][:, 3:515], in_=banks[k][:], func=AF.Copy), reads=[b_bank[k]], writes=[b_pre[pi]])
                            ck = cv_banks[xi % 2]
                            for i4 in range(4):
                                S.op("pe", lambda e, ck=ck, i4=i4, pi=pi, di=di: e.matmul(banks[ck][:], lhsT=dg[di][:, i4, :], rhs=pre[pi][:, i4:i4 + 512], start=(i4 == 0), stop=(i4 == 3)),
                                     reads=[b_dg[di], b_pre[pi]], writes=[b_bank[ck]], skip_self=True)
                            S.op("act", lambda e, ck=ck, xi=xi, ch=ch: e.activation(out=xact[xi][:], in_=banks[ck][:], func=AF.Silu, bias=cbc[:, ch:ch + 1]), reads=[b_bank[ck], b_cbc], writes=[b_xact[xi]])
                            if ch >= 16:
                                S.dma("sp", lambda e, xi=xi, ch=ch, tt=tt: e.dma_start(out=bcT_d[ch - 16, :, tt * 512:(tt + 1) * 512], in_=xact[xi][:]), reads=[b_xact[xi]], writes=[b_bcT_d[tt]])
                            if ch < 20:
                                tk = tr_banks[xi % 2]
                                tpv = banks[tk].bitcast(BF16)
                                for j in range(4):
                                    S.op("pe", lambda e, j=j, tpv=tpv, xi=xi: e.transpose(out=tpv[:, j * 128:(j + 1) * 128], in_=xact[xi][:, j * 128:(j + 1) * 128], identity=ident[:]),
                                         reads=[b_xact[xi], b_ident], writes=[b_bank[tk]], skip_self=True)
                                S.op("dve", lambda e, tpv=tpv, xi=xi: e.tensor_copy(out=xst[xi][:], in_=tpv[:, 0:512].rearrange("p (j n) -> p j n", j=4)), reads=[b_bank[tk]], writes=[b_xst[xi]])
                                if ch < 16:
                                    S.dma("sp", lambda e, xi=xi, ch=ch, tt=tt: e.dma_start(out=xtok_d[tt * 512:(tt + 1) * 512, ch * 128:(ch + 1) * 128].rearrange("(j p) n -> p j n", p=128), in_=xst[xi][:]),
                                          reads=[b_xst[xi]], writes=[b_xtok_d[tt]])
                                else:
                                    S.dma("sp", lambda e, xi=xi, ch=ch, tt=tt: e.dma_start(out=bmtok_d[tt * 512:(tt + 1) * 512, (ch - 16) * 128:(ch - 15) * 128].rearrange("(j p) n -> p j n", p=128), in_=xst[xi][:]),
                                          reads=[b_xst[xi]], writes=[b_bmtok_d[tt]])
                return f

            kvb = OFF_KV
            groups = []
            for cg in range(2):
                groups.append(([(OFF_Q + cg * 512, 512, [0])], g_q(cg)))
            groups.append(([(kvb + 0, 256, [0]), (kvb + 256, 64, [256, 320]), (kvb + 320, 64, [384, 448])], g_kv1))
            groups.append(([(kvb + 512, 64, [0, 64]), (kvb + 576, 64, [128, 192]), (kvb + 384, 128, [256]), (kvb + 640, 128, [384])], g_kv2))
            groups.append(([(OFF_G, 48, [0]), (OFF_DT, 32, [48])], g_small))
            for zb in range(4):
                groups.append(([(OFF_Z + zb * 512, 512, [0])], g_z(zb)))
            for cg in range(4):
                groups.append(([(OFF_MG + cg * 512, 512, [0])], g_mg(cg)))
            for cg in range(6):
                groups.append(([(OFF_XBC + cg * 512, 512, [0])], g_xbc(cg)))
            ngroups = int(len(groups) if stop_after >= 2 else max(1, round((stop_after - 1) * len(groups))))
            cur = load_w(groups[0][0])
            for gi in range(ngroups):
                nxt = load_w(groups[gi + 1][0]) if gi + 1 < ngroups else None
                groups[gi][1](*cur)
                cur = nxt
            dbg_dump("dbg_gates", gates_sb[:], [128, NST, 48], F32, b_gates)
            dbg_dump("dbg_dt", dt_sb[:], [128, NST, 32], F32, b_dt)
            dbg_dump("dbg_ksT0", ksT[0][:], [128, S_LEN], BF16, b_ksT[0])
            dbg_dump("dbg_kwT1", kwT[1][:], [128, S_LEN], BF16, b_kwT[1])
            dbg_dump("dbg_vs", vs_aug[:], [128, NST, 2, 65], BF16, b_vs)
            S.barrier()

        if stop_after <= 2:
            S.wait_all("sp", final_evs)
            S.emit()
            return nc

        ph.close()

        p4_tts = list(range(NT)) if stop_after >= 4 else list(range(int(round((stop_after - 3) * NT))))
        if p4_only is not None:
            p4_tts = list(p4_only)
        with ExitStack() as pa:
            kcmpT = [T("kcmpT%d" % g, [128, 256], BF16, pa) for g in range(2)]; b_kcmpT = [Buf(), Buf()]
            vcaug = T("vcaug", [128, 2, 2, 128], BF16, pa); b_vcaug = Buf()
            emat = T("emat", [128, S_LEN], BF16, pa); b_emat = Buf()
            selb = T("selb", [128, NST, 64], F32, pa); b_selb = Buf()
            zer = T("zer", [128, 512], BF16, pa); b_zer = Buf()
            S.dma("sp", lambda e: e.dma_start(out=emat[:], in_=CN["emat"]), writes=[b_emat])
            S.dma("sp", lambda e: e.dma_start(out=selb[:], in_=CN["selbias"]), writes=[b_selb])
            S.op("pool", lambda e: e.memset(zer[:], 0.0), writes=[b_zer])
            S.op("pool", lambda e: e.memset(vcaug[:], 0.0), writes=[b_vcaug])
            S.op("pool", lambda e: e.memset(kcmpT[0][:], 0.0), writes=[b_kcmpT[0]])
            S.op("pool", lambda e: e.memset(kcmpT[1][:], 0.0), writes=[b_kcmpT[1]])
            with ExitStack() as p3:
                srcT = [T("p3src%d" % i, [128, S_LEN], BF16, p3) for i in range(2)]; b_src = [Buf(), Buf()]
                S.dma("sp", lambda e: e.dma_start(out=srcT[0][:], in_=kcvc_d[0]), reads=[b_kcvc_d[0]], writes=[b_src[0]])
                S.dma("sp", lambda e: e.dma_start(out=srcT[1][:], in_=kcvc_d[1]), reads=[b_kcvc_d[1]], writes=[b_src[1]])
                w1f = T("p3w1f", [128, 16, 256], F32, p3); b_w1f = Buf()
                w1b = T("p3w1b", [128, 32, 256], BF16, p3); b_w1b = Buf()
                w2f = T("p3w2f", [128, 2, 64], F32, p3); b_w2f = Buf()
                w2d = T("p3w2d", [128, 2, 128], BF16, p3); b_w2d = Buf()
                pe2 = T("p3pe2", [32, 128], F32, p3); b_pe2 = Buf()
                peT = T("p3peT", [128, 32], BF16, p3); b_peT = Buf()
                hidb = T("p3hidb", [128, 2], F32, p3); b_hidb = Buf()
                hid = [T("p3hid%d" % i, [128, 256], BF16, p3) for i in range(2)]; b_hid = [Buf(), Buf()]
                bmp = T("p3bmp", [128, 2, 64], BF16, p3); b_bmp = Buf()
                S.dma("sp", lambda e: e.dma_start(out=bmp[:], in_=CN["bmap"]), writes=[b_bmp])
                for kv, nm in enumerate(("k", "v")):
                    w1v = I["cmp_w1_" + nm].rearrange("(l d) h -> d l h", d=64)
                    for lh in range(2):
                        for half in range(2):
                            S.dma("sp", lambda e, half=half, lh=lh, w1v=w1v: e.dma_start(out=w1f[half * 64:(half + 1) * 64, :, :], in_=w1v[:, lh * 16:(lh + 1) * 16, :]), writes=[b_w1f])
                        S.op("pool", lambda e, lh=lh: e.tensor_copy(out=w1b[:, lh * 16:(lh + 1) * 16, :], in_=w1f[:]), reads=[b_w1f], writes=[b_w1b])
                    S.dma("sp", lambda e, nm=nm: e.dma_start(out=w2f[:], in_=I["cmp_w2_" + nm].rearrange("(c p) d -> p c d", p=128)), writes=[b_w2f])
                    S.op("pool", lambda e: e.tensor_copy(out=w2d[:, :, 0:64], in_=w2f[:]), reads=[b_w2f], writes=[b_w2d])
                    S.op("pool", lambda e: e.tensor_copy(out=w2d[:, :, 64:128], in_=w2f[:]), reads=[b_w2f], writes=[b_w2d])
                    S.dma("sp", lambda e, nm=nm: e.dma_start(out=pe2[:, 0:64], in_=I["cmp_pe_" + nm]), writes=[b_pe2])
                    S.dma("sp", lambda e, nm=nm: e.dma_start(out=pe2[:, 64:128], in_=I["cmp_pe_" + nm]), writes=[b_pe2])
                    S.op("pe", lambda e: e.transpose(out=banks[7][:, 0:32], in_=pe2[:], identity=identf[0:32, 0:32]), reads=[b_pe2, b_identf], writes=[b_bank[7]])
                    S.op("dve", lambda e: e.tensor_copy(out=peT[:], in_=banks[7][:, 0:32]), reads=[b_bank[7]], writes=[b_peT])
                    for hc in range(2):
                        for l in range(32):
                            S.op("pe", lambda e, hc=hc, l=l: e.matmul(banks[6][:, hc:hc + 1], lhsT=w1b[0:64, l, hc * 128:(hc + 1) * 128], rhs=peT[0:64, l:l + 1], start=(l == 0), stop=(l == 31)),
                                 reads=[b_w1b, b_peT], writes=[b_bank[6]], skip_self=True)
                    S.op("dve", lambda e: e.tensor_copy(out=hidb[:], in_=banks[6][:, 0:2]), reads=[b_bank[6]], writes=[b_hidb])
                    for g in range(2):
                        gs = slice(g * 64, (g + 1) * 64)
                        for hc in range(2):
                            bk = hc
                            for l in range(32):
                                S.op("pe", lambda e, hc=hc, l=l, gs=gs, bk=bk, kv=kv: e.matmul(banks[bk][:, 0:255], lhsT=w1b[gs, l, hc * 128:(hc + 1) * 128], rhs=srcT[kv][gs, l:l + 16 * 254 + 1:16], start=(l == 0), stop=(l == 31)),
                                     reads=[b_w1b, b_src[kv]], writes=[b_bank[bk]], skip_self=True)
                            S.op("act", lambda e, hc=hc, bk=bk: e.activation(out=hid[hc][:, 0:255], in_=banks[bk][:, 0:255], func=AF.Silu, bias=hidb[:, hc:hc + 1]), reads=[b_bank[bk], b_hidb], writes=[b_hid[hc]])
                        if kv == 0:
                            for hc in range(2):
                                S.op("pe", lambda e, hc=hc: e.matmul(banks[2][:, 0:255], lhsT=w2d[:, hc, :], rhs=hid[hc][:, 0:255], start=(hc == 0), stop=(hc == 1)),
                                     reads=[b_w2d, b_hid[hc]], writes=[b_bank[2]], skip_self=True)
                            S.op("act", lambda e, g=g: e.activation(out=kcmpT[g][:, 0:255], in_=banks[2][:, 0:255], func=AF.Copy), reads=[b_bank[2]], writes=[b_kcmpT[g]])
                        else:
                            for nt_ in range(2):
                                cnt = 128 if nt_ == 0 else 127
                                for hc in range(2):
                                    S.op("pe", lambda e, hc=hc, nt_=nt_, cnt=cnt: e.matmul(banks[3][0:cnt, nt_ * 64:(nt_ + 1) * 64], lhsT=hid[hc][:, nt_ * 128:nt_ * 128 + cnt], rhs=w2d[:, hc, 0:64], start=(hc == 0), stop=(hc == 1)),
                                         reads=[b_w2d, b_hid[hc]], writes=[b_bank[3]], skip_self=True)
                                S.op("act", lambda e, g=g, nt_=nt_, cnt=cnt: e.activation(out=vcaug[0:cnt, nt_, g, 0:64], in_=banks[3][0:cnt, nt_ * 64:(nt_ + 1) * 64], func=AF.Copy), reads=[b_bank[3]], writes=[b_vcaug])
                for g in range(2):
                    S.op("pool", lambda e, g=g: e.tensor_copy(out=vcaug[:, :, g, 64:128], in_=bmp[:]), reads=[b_bmp], writes=[b_vcaug])
                dbg_dump("dbg_kcmpT0", kcmpT[0][:], [128, 256], BF16, [b_kcmpT[0]])
                dbg_dump("dbg_vcaug", vcaug[:], [128, 2, 2, 128], BF16, [b_vcaug])
                S.barrier()

            if stop_after <= 3:
                S.wait_all("sp", final_evs)
                S.emit()
                return nc

            with ExitStack() as p4:
                wao = T("p4wao", [128, 8, D], BF16, p4); b_wao = Buf()
                fsA = [T("p4fs%d" % i, [128, 8, 256], F32, p4) for i in range(2)]; b_fsA = [Buf(), Buf()]
                wao_v = I["w_attn_out"].rearrange("(kc p) n -> p kc n", p=128)
                for i in range(4):
                    S.dma("sp", lambda e, i=i: e.dma_start(out=fsA[i % 2][:], in_=wao_v[:, :, i * 256:(i + 1) * 256]), writes=[b_fsA[i % 2]])
                    S.op("pool", lambda e, i=i: e.tensor_copy(out=wao[:, :, i * 256:(i + 1) * 256], in_=fsA[i % 2][:]), reads=[b_fsA[i % 2]], writes=[b_wao])
                qt = [T("p4qt%d" % i, [128, 8, 512], BF16, p4) for i in range(2)]; b_qt = [Buf(), Buf()]
                NP = 5
                pt = [T("p4pt%d" % i, [128, 512], BF16, p4) for i in range(NP)]; b_pt = [Buf() for _ in range(NP)]
                oacc = T("p4oacc", [128, 4, D], F32, p4); b_oacc = Buf()
                obf = T("p4obf", [128, 4, D], BF16, p4); b_obf = Buf()
                oT = T("p4oT", [128, 8, 512], BF16, p4); b_oT = Buf()
                impa = [T("p4impa%d" % g, [128, 4, 64], F32, p4) for g in range(2)]; b_impa = [Buf(), Buf()]
                impt = T("p4impt", [128, 4, 64], F32, p4); b_impt = Buf()
                impr = T("p4impr", [128, 4, 64], F32, p4); b_impr = Buf()
                m8 = T("p4m8", [128, 4, 8], F32, p4); b_m8 = Buf()
                nst_ = T("p4nst", [128, 4, 128], BF16, p4); b_nst = Buf()
                negT = [T("p4negT%d" % g, [128, 512], BF16, p4) for g in range(2)]; b_negT = [Buf(), Buf()]
                den = [T("p4den%d" % i, [128, 4], F32, p4) for i in range(4)]; b_den = [Buf() for _ in range(4)]
                otmp = [T("p4otmp%d" % i, [128, 4, 64], F32, p4) for i in range(2)]; b_otmp = [Buf(), Buf()]
                ystg = [T("p4ystg%d" % i, [128, 512], BF16, p4) for i in range(2)]; b_ystg = [Buf(), Buf()]
                ST_B = [0, 1, 2]
                LOOK = 2
                dctr = [0]

                def gate_ap(tt, h, br):
                    return gates_sb[:, tt * 4:(tt + 1) * 4, h * 3 + br]

                def run_tiles(tiles):
                    n = len(tiles)
                    for idx in range(n + LOOK):
                        if idx < n:
                            tl = tiles[idx]
                            bk = ST_B[idx % 3]; pi = idx % NP
                            tl["score"](bk)
                            tl["post"](bk, pi)
                        j = idx - LOOK
                        if j >= 0:
                            tl = tiles[j]
                            if tl.get("pre"):
                                tl["pre"]()
                            tl["pv"](j % NP)
                            if tl.get("fin"):
                                tl["fin"]()

                for tt in p4_tts:
                    q0 = tt * 512
                    qi = tt % 2
                    S.dma("sp", lambda e, qi=qi, q0=q0: e.dma_start(out=qt[qi][:], in_=qT_d[:, :, q0:q0 + 512].rearrange("c p t -> p c t")),
                          reads=[b_qT_d[c][tt] for c in range(8)], writes=[b_qt[qi]])
                    st4 = list(range(tt * 4, tt * 4 + 4))
                    tilesA = []
                    nmax = 32 * tt + 30
                    for h in range(16):
                        g = h // 8; c = h // 2; es = slice((h % 2) * 64, (h % 2) * 64 + 64)
                        nts = [0] if nmax < 128 else [0, 1]
                        for nt_ in nts:
                            cnt = min(128, nmax + 1 - 128 * nt_)
                            full_valid = (16 * (128 * nt_ + cnt - 1) + 31) <= q0

                            def score(bk, g=g, c=c, es=es, nt_=nt_, cnt=cnt):
                                S.op("pe", lambda e: e.matmul(banks[bk][0:cnt, :], lhsT=kcmpT[g][es, nt_ * 128:nt_ * 128 + cnt], rhs=qt[qi][es, c, :], start=True, stop=True),
                                     reads=[b_kcmpT[g], b_qt[qi]], writes=[b_bank[bk]])

                            def post(bk, pi, cnt=cnt, nt_=nt_, full_valid=full_valid):
                                S.op("act", lambda e: e.activation(out=pt[pi][0:cnt, :], in_=banks[bk][0:cnt, :], func=AF.Exp, scale=0.125), reads=[b_bank[bk]], writes=[b_pt[pi]])
                                if not full_valid:
                                    S.op("pool", lambda e: e.affine_select(out=pt[pi][0:cnt, :], in_=pt[pi][0:cnt, :], pattern=[[1, 512]], compare_op=ALU.is_ge, fill=0.0,
                                                                            base=q0 - 16 * 128 * nt_ - 31, channel_multiplier=-16), reads=[b_pt[pi]], writes=[b_pt[pi]])

                            def pre(first=(nt_ == nts[0])):
                                if first:
                                    S.op("pe", lambda e: e.matmul(banks[3][:, :], lhsT=zer[:, 0:128], rhs=zer[:, 0:512], start=True, stop=True), reads=[b_zer], writes=[b_bank[3]])

                            def pv(pi, cnt=cnt, nt_=nt_, g=g):
                                for j4 in range(4):
                                    S.op("pe", lambda e, j4=j4: e.matmul(banks[3][:, j4 * 128:(j4 + 1) * 128], lhsT=pt[pi][0:cnt, j4 * 128:(j4 + 1) * 128], rhs=vcaug[0:cnt, nt_, g, :], start=False, stop=True, skip_group_check=True),
                                         reads=[b_pt[pi], b_vcaug], writes=[b_bank[3]], skip_self=True)

                            def fin(h=h, g=g, last=(nt_ == nts[-1])):
                                if not last:
                                    return
                                acc = banks[3][:, :].rearrange("p (j n) -> p j n", j=4)
                                di = dctr[0] % 4; dctr[0] += 1
                                S.op("dve", lambda e: e.tensor_reduce(out=den[di][:], in_=acc[:, :, 64:128], axis=AX.X, op=ALU.add), reads=[b_bank[3]], writes=[b_den[di]])
                                S.op("dve", lambda e: e.tensor_scalar(out=den[di][:], in0=den[di][:], scalar1=1e-30, scalar2=None, op0=ALU.max), reads=[b_den[di]], writes=[b_den[di]])
                                S.op("dve", lambda e: e.reciprocal(out=den[di][:], in_=den[di][:]), reads=[b_den[di]], writes=[b_den[di]])
                                if h % 8 == 0:
                                    S.op("dve", lambda e: e.tensor_tensor(out=impa[g][:], in0=acc[:, :, 64:128], in1=den[di][:, :, None].to_broadcast([128, 4, 64]), op=ALU.mult), reads=[b_bank[3], b_den[di]], writes=[b_impa[g]])
                                else:
                                    S.op("dve", lambda e: e.tensor_tensor(out=impt[:], in0=acc[:, :, 64:128], in1=den[di][:, :, None].to_broadcast([128, 4, 64]), op=ALU.mult), reads=[b_bank[3], b_den[di]], writes=[b_impt])
                                    S.op("pool", lambda e: e.tensor_tensor(out=impa[g][:], in0=impa[g][:], in1=impt[:], op=ALU.add), reads=[b_impt, b_impa[g]], writes=[b_impa[g]])
                                S.op("dve", lambda e: e.tensor_tensor(out=den[di][:], in0=den[di][:], in1=gate_ap(tt, h, 0), op=ALU.mult), reads=[b_den[di]] + [b_gates[s_] for s_ in st4], writes=[b_den[di]])
                                S.op("dve", lambda e: e.tensor_tensor(out=oacc[:, :, h * 64:(h + 1) * 64], in0=acc[:, :, 0:64], in1=den[di][:, :, None].to_broadcast([128, 4, 64]), op=ALU.mult), reads=[b_bank[3], b_den[di]], writes=[b_oacc])
                                if h % 8 == 7:
                                    S.op("dve", lambda e: e.tensor_tensor(out=impr[:], in0=impa[g][:], in1=selb[:, tt * 4:(tt + 1) * 4, :], op=ALU.add), reads=[b_impa[g], b_selb], writes=[b_impr])
                                    for j4 in range(4):
                                        S.op("dve", lambda e, j4=j4: e.max(out=m8[:, j4, :], in_=impr[:, j4, :]), reads=[b_impr], writes=[b_m8])
                                        S.op("dve", lambda e, j4=j4: e.match_replace(out=impt[:, j4, :], in_to_replace=m8[:, j4, :], in_values=impr[:, j4, :], imm_value=-1e30), reads=[b_impr, b_m8], writes=[b_impt])
                                        S.op("dve", lambda e, j4=j4: e.max(out=m8[:, j4, :], in_=impt[:, j4, :]), reads=[b_impt], writes=[b_m8])
                                        S.op("dve", lambda e, j4=j4: e.tensor_scalar(out=nst_[:, j4, 0:64], in0=impr[:, j4, :], scalar1=m8[:, j4, 7:8], scalar2=NEG, op0=ALU.is_lt, op1=ALU.mult), reads=[b_impr, b_m8], writes=[b_nst])
                                        S.op("dve", lambda e, j4=j4: e.tensor_copy(out=nst_[:, j4, 64:128], in_=nst_[:, j4, 0:64]), reads=[b_nst], writes=[b_nst])
                                    tpv = banks[7].bitcast(BF16)
                                    for j4 in range(4):
                                        S.op("pe", lambda e, j4=j4: e.transpose(out=tpv[:, j4 * 128:(j4 + 1) * 128], in_=nst_[:, j4, :], identity=ident[:]), reads=[b_nst, b_ident], writes=[b_bank[7]], skip_self=True)
                                    S.op("act", lambda e: e.activation(out=negT[g][:], in_=tpv[:, 0:512], func=AF.Copy), reads=[b_bank[7]], writes=[b_negT[g]])
                                    if ("dbg_negsel%d" % tt) in dbg and g == 1:
                                        pass

                            tilesA.append(dict(score=score, post=post, pre=pre, pv=pv, fin=fin))
                    run_tiles(tilesA)
                    if "dbg_neg" in dbg and tt == p4_tts[-1]:
                        dbg_dump("dbg_neg", negT[0][:], [128, 512], BF16, [b_negT[0]])
                        dbg_dump("dbg_impa", impa[0][:], [128, 4, 64], F32, [b_impa[0]])

                    tilesB = []
                    for h in range(16):
                        g = h // 8; c = h // 2; es = slice((h % 2) * 64, (h % 2) * 64 + 64)
                        sb_ = 3 + (h % 2); wb_ = 5 + (h % 2)
                        kts = list(range(0, 4 * tt + 4))
                        for kt in kts:
                            jd = kt - 4 * tt
                            n0 = 128 * jd if jd > 0 else 0

                            def score(bk, g=g, c=c, es=es, kt=kt, n0=n0):
                                S.op("pe", lambda e: e.matmul(banks[bk][:, n0:512], lhsT=ksT[g][es, kt * 128:(kt + 1) * 128], rhs=qt[qi][es, c, n0:512], start=True, stop=False),
                                     reads=[b_ksT[g][kt // 4], b_qt[qi]], writes=[b_bank[bk]])
                                S.op("pe", lambda e: e.matmul(banks[bk][:, n0:512], lhsT=emat[es, kt * 128:(kt + 1) * 128], rhs=negT[g][es, n0:512], start=False, stop=True),
                                     reads=[b_emat, b_negT[g]], writes=[b_bank[bk]], skip_self=True)

                            def post(bk, pi, kt=kt, n0=n0, jd=jd):
                                S.op("act", lambda e: e.activation(out=pt[pi][:, n0:512], in_=banks[bk][:, n0:512], func=AF.Exp, scale=0.125), reads=[b_bank[bk]], writes=[b_pt[pi]])
                                if jd >= 0:
                                    S.op("pool", lambda e: e.affine_select(out=pt[pi][:, n0:n0 + 128], in_=pt[pi][:, n0:n0 + 128], pattern=[[1, 128]], compare_op=ALU.is_ge, fill=0.0,
                                                                            base=0, channel_multiplier=-1), reads=[b_pt[pi]], writes=[b_pt[pi]])

                            def pre(first=(kt == 0), sb_=sb_):
                                if first:
                                    S.op("pe", lambda e: e.matmul(banks[sb_][:, 0:260], lhsT=zer[:, 0:128], rhs=zer[:, 0:260], start=True, stop=True), reads=[b_zer], writes=[b_bank[sb_]])

                            def pv(pi, kt=kt, n0=n0, g=g, sb_=sb_):
                                for j4 in range(n0 // 128, 4):
                                    S.op("pe", lambda e, j4=j4: e.matmul(banks[sb_][:, j4 * 65:(j4 + 1) * 65], lhsT=pt[pi][:, j4 * 128:(j4 + 1) * 128], rhs=vs_aug[:, kt, g, :], start=False, stop=True, skip_group_check=True),
                                         reads=[b_pt[pi], b_vs[kt]], writes=[b_bank[sb_]], skip_self=True)

                            tilesB.append(dict(score=score, post=post, pre=pre, pv=pv, fin=None))
                        jw = [j for j in range(8) if 4 * tt - 4 + j >= 0]
                        for j in jw:
                            kt = 4 * tt - 4 + j
                            n0, n1 = (0, 128 * (j + 1)) if j <= 3 else (128 * (j - 4), 512)

                            def score(bk, g=g, c=c, es=es, kt=kt, n0=n0, n1=n1):
                                S.op("pe", lambda e: e.matmul(banks[bk][:, n0:n1], lhsT=kwT[g][es, kt * 128:(kt + 1) * 128], rhs=qt[qi][es, c, n0:n1], start=True, stop=True),
                                     reads=[b_kwT[g][kt // 4], b_qt[qi]], writes=[b_bank[bk]])

                            def post(bk, pi, j=j, n0=n0, n1=n1):
                                S.op("act", lambda e: e.activation(out=pt[pi][:, n0:n1], in_=banks[bk][:, n0:n1], func=AF.Exp, scale=0.125), reads=[b_bank[bk]], writes=[b_pt[pi]])
                                if j <= 3:
                                    m0 = 128 * j
                                    S.op("pool", lambda e: e.affine_select(out=pt[pi][:, m0:m0 + 128], in_=pt[pi][:, m0:m0 + 128], pattern=[[-1, 128]], compare_op=ALU.is_ge, fill=0.0,
                                                                            base=-1, channel_multiplier=1), reads=[b_pt[pi]], writes=[b_pt[pi]])
                                else:
                                    m0 = 128 * (j - 4)
                                    S.op("pool", lambda e: e.affine_select(out=pt[pi][:, m0:m0 + 128], in_=pt[pi][:, m0:m0 + 128], pattern=[[1, 128]], compare_op=ALU.is_ge, fill=0.0,
                                                                            base=0, channel_multiplier=-1), reads=[b_pt[pi]], writes=[b_pt[pi]])

                            def pre(first=(j == jw[0]), wb_=wb_):
                                if first:
                                    S.op("pe", lambda e: e.matmul(banks[wb_][:, 0:260], lhsT=zer[:, 0:128], rhs=zer[:, 0:260], start=True, stop=True), reads=[b_zer], writes=[b_bank[wb_]])

                            def pv(pi, kt=kt, n0=n0, n1=n1, g=g, wb_=wb_):
                                for j4 in range(n0 // 128, n1 // 128):
                                    S.op("pe", lambda e, j4=j4: e.matmul(banks[wb_][:, j4 * 65:(j4 + 1) * 65], lhsT=pt[pi][:, j4 * 128:(j4 + 1) * 128], rhs=vw_aug[:, kt, g, :], start=False, stop=True, skip_group_check=True),
                                         reads=[b_pt[pi], b_vw[kt]], writes=[b_bank[wb_]], skip_self=True)

                            def fin(h=h, last=(j == jw[-1]), sb_=sb_, wb_=wb_):
                                if not last:
                                    return
                                for (bank_, br) in ((sb_, 1), (wb_, 2)):
                                    acc = banks[bank_][:, 0:260].rearrange("p (j n) -> p j n", j=4)
                                    di = dctr[0] % 4; dctr[0] += 1
                                    oi = dctr[0] % 2
                                    S.op("dve", lambda e, acc=acc, di=di: e.reciprocal(out=den[di][:], in_=acc[:, :, 64]), reads=[b_bank[bank_]], writes=[b_den[di]])
                                    S.op("dve", lambda e, di=di, br=br: e.tensor_tensor(out=den[di][:], in0=den[di][:], in1=gate_ap(tt, h, br), op=ALU.mult), reads=[b_den[di]] + [b_gates[s_] for s_ in st4], writes=[b_den[di]])
                                    S.op("dve", lambda e, acc=acc, di=di, oi=oi: e.tensor_tensor(out=otmp[oi][:], in0=acc[:, :, 0:64], in1=den[di][:, :, None].to_broadcast([128, 4, 64]), op=ALU.mult), reads=[b_bank[bank_], b_den[di]], writes=[b_otmp[oi]])
                                    S.op("pool", lambda e, oi=oi: e.tensor_tensor(out=oacc[:, :, h * 64:(h + 1) * 64], in0=oacc[:, :, h * 64:(h + 1) * 64], in1=otmp[oi][:], op=ALU.add), reads=[b_otmp[oi], b_oacc], writes=[b_oacc])

                            tilesB.append(dict(score=score, post=post, pre=pre, pv=pv, fin=fin))
                    run_tiles(tilesB)

                    if "dbg_oacc" in dbg:
                        d_ = nc.dram_tensor("dbg_oacc%d" % tt, [128, 4, D], F32, kind="ExternalOutput").ap()
                        final_evs.append(S.dma("sp", lambda e, d_=d_: e.dma_start(out=d_, in_=oacc[:]), reads=[b_oacc]))
                    S.op("act", lambda e: e.activation(out=obf[:], in_=oacc[:], func=AF.Copy), reads=[b_oacc], writes=[b_obf])
                    tpv = banks[7].bitcast(BF16)
                    for kc in range(8):
                        for j4 in range(4):
                            S.op("pe", lambda e, kc=kc, j4=j4: e.transpose(out=tpv[:, j4 * 128:(j4 + 1) * 128], in_=obf[:, j4, kc * 128:(kc + 1) * 128], identity=ident[:]), reads=[b_obf, b_ident], writes=[b_bank[7]], skip_self=True)
                        S.op("dve", lambda e, kc=kc: e.tensor_copy(out=oT[:, kc, :], in_=tpv[:, 0:512]), reads=[b_bank[7]], writes=[b_oT])
                    for oc in range(8):
                        bk = ST_B[oc % 3]
                        for kc in range(8):
                            S.op("pe", lambda e, kc=kc, oc=oc, bk=bk: e.matmul(banks[bk][:], lhsT=wao[:, kc, oc * 128:(oc + 1) * 128], rhs=oT[:, kc, :], start=(kc == 0), stop=(kc == 7)),
                                 reads=[b_wao, b_oT], writes=[b_bank[bk]], skip_self=True)
                        yi = oc % 2
                        S.op("act", lambda e, bk=bk, yi=yi: e.activation(out=ystg[yi][:], in_=banks[bk][:], func=AF.Copy), reads=[b_bank[bk]], writes=[b_ystg[yi]])
                        S.dma("sp", lambda e, yi=yi, oc=oc, q0=q0: e.dma_start(out=yaT_d[oc, :, q0:q0 + 512], in_=ystg[yi][:]), reads=[b_ystg[yi]], writes=[b_yaT_d[tt]])
                S.barrier()

        if stop_after <= 4:
            S.wait_all("sp", final_evs)
            S.emit()
            return nc

        pk.close()

        p5_chunks = NST if stop_after >= 5 else int(round((stop_after - 4) * NST))
        if "p5" in skip:
            p5_chunks = 0
        with ExitStack() as p5:
            utri = T("utri", [128, 128], F32, p5); b_utri = Buf()
            negtri = T("negtri", [128, 128], F32, p5); b_negtri = Buf()
            S.dma("sp", lambda e: e.dma_start(out=utri[:], in_=CN["utri"]), writes=[b_utri])
            S.dma("sp", lambda e: e.dma_start(out=negtri[:], in_=CN["negtri"]), writes=[b_negtri])
            A_b = T("A_b", [128, 32], F32, p5); b_Ab = Buf()
            D_b = T("D_b", [128, 32], F32, p5); b_Db = Buf()
            nw_b = T("nw_b", [128, 2048], F32, p5); b_nwb = Buf()
            S.dma("sp", lambda e: e.dma_start(out=A_b[:], in_=I["a_log"].partition_broadcast(128)), writes=[b_Ab])
            S.dma("sp", lambda e: e.dma_start(out=D_b[:], in_=I["d_skip"].partition_broadcast(128)), writes=[b_Db])
            S.dma("sp", lambda e: e.dma_start(out=nw_b[:], in_=I["ssm_norm_w"].partition_broadcast(128)), writes=[b_nwb])
            S.op("act", lambda e: e.activation(out=A_b[:], in_=A_b[:], func=AF.Exp), reads=[b_Ab], writes=[b_Ab])
            S.op("dve", lambda e: e.tensor_scalar(out=A_b[:], in0=A_b[:], scalar1=-1.0, scalar2=None, op0=ALU.mult), reads=[b_Ab], writes=[b_Ab])
            wso = T("p5wso", [128, 16, D], BF16, p5); b_wso = Buf()
            fsS = [T("p5fs%d" % i, [128, 8, 256], F32, p5) for i in range(2)]; b_fsS = [Buf(), Buf()]
            wso_v = I["w_ssm_out"].rearrange("(kc p) n -> p kc n", p=128)
            ii = 0
            for kh in range(2):
                for i in range(4):
                    S.dma("sp", lambda e, i=i, kh=kh, ii=ii: e.dma_start(out=fsS[ii % 2][:], in_=wso_v[:, kh * 8:(kh + 1) * 8, i * 256:(i + 1) * 256]), writes=[b_fsS[ii % 2]])
                    S.op("pool", lambda e, i=i, kh=kh, ii=ii: e.tensor_copy(out=wso[:, kh * 8:(kh + 1) * 8, i * 256:(i + 1) * 256], in_=fsS[ii % 2][:]), reads=[b_fsS[ii % 2]], writes=[b_wso])
                    ii += 1
            HT = T("p5HT", [128, 2048], F32, p5); b_HT = [Buf() for _ in range(4)]
            HTb = T("p5HTb", [128, 2048], BF16, p5); b_HTb = [Buf() for _ in range(4)]
            xin = [T("p5x%d" % i, [128, 2048], BF16, p5) for i in range(2)]; b_xin = [Buf(), Buf()]
            zin = [T("p5z%d" % i, [128, 2048], BF16, p5) for i in range(2)]; b_zin = [Buf(), Buf()]
            bmin = [T("p5bm%d" % i, [128, 512], BF16, p5) for i in range(2)]; b_bmin = [Buf(), Buf()]
            bcin = [T("p5bc%d" % i, [128, 8, 128], BF16, p5) for i in range(2)]; b_bcin = [Buf(), Buf()]
            dta = T("p5dta", [128, 128], F32, p5); b_dta = Buf()
            S.op("dve", lambda e: e.memset(dta[:], 0.0), writes=[b_dta])
            acum = T("p5acum", [128, 32], F32, p5); b_acum = Buf()
            acumT = T("p5acumT", [128, 128], F32, p5); b_acumT = Buf()
            alast = T("p5alast", [128, 32], F32, p5); b_alast = Buf()
            dout = T("p5dout", [128, 32], F32, p5); b_dout = Buf()
            ea = T("p5ea", [128, 32], F32, p5); b_ea = Buf()
            cdb = T("p5cdb", [128, 32], F32, p5); b_cdb = Buf()
            dtmp = [T("p5dtmp%d" % i, [128, 8, 128], F32, p5) for i in range(2)]; b_dtmp = [Buf(), Buf()]
            MT = [T("p5MT%d" % i, [128, 8, 128], BF16, p5) for i in range(2)]; b_MT = [Buf(), Buf()]
            xdt = T("p5xdt", [128, 2048], BF16, p5); b_xdt = Buf()
            xw = T("p5xw", [128, 2048], BF16, p5); b_xw = Buf()
            yt1 = T("p5yt1", [128, 2048], F32, p5); b_yt1 = [Buf() for _ in range(4)]
            yt2 = T("p5yt2", [128, 2048], F32, p5); b_yt2 = Buf()
            sq5 = T("p5sq", [128, 512], BF16, p5); b_sq5 = Buf()
            ss5 = T("p5ss", [128, 4], F32, p5); b_ss5 = Buf()
            osb = T("p5osb", [128, 2048], BF16, p5); b_osb = Buf()
            osT = T("p5osT", [128, 16, 512], BF16, p5); b_osT = Buf()
            htmp = T("p5htmp", [128, 512], F32, p5); b_htmp = Buf()
            ystg5 = [T("p5ystg%d" % i, [128, 512], BF16, p5) for i in range(2)]; b_ystg5 = [Buf(), Buf()]

            def bc3(ap2, n):
                return ap2.unsqueeze(2).to_broadcast([128, ap2.shape[1], n])

            for ci in (range(p5_chunks) if p5_list is None else p5_list):
                tt = ci // 4; j4 = ci % 4; bi = ci % 2
                r0 = ci * 128
                S.dma("sp", lambda e, bi=bi, r0=r0: e.dma_start(out=xin[bi][:], in_=xtok_d[r0:r0 + 128, :]), reads=[b_xtok_d[tt]], writes=[b_xin[bi]])
                S.dma("sp", lambda e, bi=bi, r0=r0: e.dma_start(out=zin[bi][:], in_=z_d[r0:r0 + 128, :]), reads=[b_z_d[tt]], writes=[b_zin[bi]])
                S.dma("sp", lambda e, bi=bi, r0=r0: e.dma_start(out=bmin[bi][:], in_=bmtok_d[r0:r0 + 128, :]), reads=[b_bmtok_d[tt]], writes=[b_bmin[bi]])
                S.dma("sp", lambda e, bi=bi, r0=r0: e.dma_start(out=bcin[bi][:], in_=bcT_d[:, :, r0:r0 + 128].rearrange("c p t -> p c t")), reads=[b_bcT_d[tt]], writes=[b_bcin[bi]])
                tsl = slice(0, 128)
                osl = slice(j4 * 128, (j4 + 1) * 128)
                xc = xin[bi][:]; zc = zin[bi][:]
                dtc = dt_sb[:, ci, :]
                S.op("dve", lambda e: e.tensor_tensor(out=dta[:, 0:32], in0=dtc, in1=A_b[:], op=ALU.mult), reads=[b_dt[ci], b_Ab], writes=[b_dta])
                S.op("pe", lambda e: e.matmul(banks[0][:, 0:32], lhsT=utri[:], rhs=dta[:, 0:32], start=True, stop=True), reads=[b_utri, b_dta], writes=[b_bank[0]])
                S.op("pe", lambda e: e.matmul(banks[0][:, 128:256], lhsT=dta[:], rhs=utri[:], start=True, stop=True), reads=[b_utri, b_dta], writes=[b_bank[0]], skip_self=True)
                S.op("dve", lambda e: e.tensor_copy(out=acum[:], in_=banks[0][:, 0:32]), reads=[b_bank[0]], writes=[b_acum])
                S.op("dve", lambda e: e.tensor_copy(out=acumT[:], in_=banks[0][:, 128:256]), reads=[b_bank[0]], writes=[b_acumT])
                S.op("act", lambda e: e.activation(out=ea[:], in_=acum[:], func=AF.Exp), reads=[b_acum], writes=[b_ea])
                if p5_cut < 1:
                    continue
                for g in range(4):
                    S.op("pe", lambda e, g=g: e.matmul(banks[5][:, g * 128:(g + 1) * 128], lhsT=bcin[bi][:, g, tsl], rhs=bcin[bi][:, 4 + g, tsl], start=True, stop=True),
                         reads=[b_bcin[bi]], writes=[b_bank[5]], skip_self=True)
                if p5_cut < 2:
                    continue
                for g in range(4):
                    gi = g % 2
                    bA, bB = (1, 2) if gi == 0 else (3, 4)
                    for hh in range(8):
                        h = g * 8 + hh
                        bk = bA if hh < 4 else bB
                        S.op("pe", lambda e, h=h, hh=hh, bk=bk: e.matmul(banks[bk][:, (hh % 4) * 128:(hh % 4 + 1) * 128], lhsT=identf[:, h:h + 1].to_broadcast([128, 128]), rhs=acumT[:], start=True, stop=True),
                             reads=[b_identf, b_acumT], writes=[b_bank[bk]], skip_self=True)
                    for hh in range(8):
                        h = g * 8 + hh
                        bk = bA if hh < 4 else bB
                        S.op("dve", lambda e, h=h, hh=hh, bk=bk, gi=gi: e.scalar_tensor_tensor(out=dtmp[gi][:, hh, :], in0=banks[bk][:, (hh % 4) * 128:(hh % 4 + 1) * 128], scalar=acum[:, h:h + 1], in1=negtri[:], op0=ALU.subtract, op1=ALU.add),
                             reads=[b_bank[bk], b_acum, b_negtri], writes=[b_dtmp[gi]])
                    for half, bk in ((0, bA), (1, bB)):
                        S.op("dve", lambda e, g=g, half=half, bk=bk: e.tensor_copy(out=alast[:, g * 8 + half * 4:g * 8 + half * 4 + 4], in_=banks[bk][:, :].rearrange("p (h l) -> p h l", h=4)[:, :, 127]),
                             reads=[b_bank[bk]], writes=[b_alast])
                    S.op("act", lambda e, gi=gi: e.activation(out=dtmp[gi][:], in_=dtmp[gi][:], func=AF.Exp), reads=[b_dtmp[gi]], writes=[b_dtmp[gi]])
                    S.op("dve", lambda e, g=g, gi=gi: e.tensor_tensor(out=MT[gi][:], in0=dtmp[gi][:], in1=banks[5][:, g * 128:(g + 1) * 128].unsqueeze(1).to_broadcast([128, 8, 128]), op=ALU.mult),
                         reads=[b_dtmp[gi], b_bank[5]], writes=[b_MT[gi]])
                    if g == 0:
                        S.op("dve", lambda e: e.tensor_tensor(out=xdt[:].rearrange("p (h d) -> p h d", h=32), in0=xc.rearrange("p (h d) -> p h d", h=32), in1=bc3(dtc, 64), op=ALU.mult),
                             reads=[b_xin[bi], b_dt[ci]], writes=[b_xdt])
                    for hh in range(8):
                        h = g * 8 + hh
                        S.op("pe", lambda e, h=h, hh=hh, gi=gi: e.matmul(banks[6][:, hh * 64:(hh + 1) * 64], lhsT=MT[gi][:, hh, :], rhs=xdt[:, h * 64:(h + 1) * 64], start=True, stop=True),
                             reads=[b_MT[gi], b_xdt], writes=[b_bank[6]], skip_self=True)
                    gsl = slice(g * 512, (g + 1) * 512)
                    if ci > 0 and not p5_nostate:
                        S.op("pe", lambda e, g=g: e.matmul(banks[7][:], lhsT=bcin[bi][:, 4 + g, tsl], rhs=HTb[:, g * 512:(g + 1) * 512], start=True, stop=True),
                             reads=[b_bcin[bi], b_HTb[g]], writes=[b_bank[7]])
                        S.op("dve", lambda e, g=g: e.tensor_tensor(out=yt1[:, g * 512:(g + 1) * 512].rearrange("p (h d) -> p h d", h=8), in0=banks[7][:].rearrange("p (h d) -> p h d", h=8), in1=bc3(ea[:, g * 8:(g + 1) * 8], 64), op=ALU.mult),
                             reads=[b_bank[7], b_ea], writes=[b_yt1[g]])
                        S.op("dve", lambda e, g=g: e.tensor_tensor(out=yt1[:, g * 512:(g + 1) * 512], in0=yt1[:, g * 512:(g + 1) * 512], in1=banks[6][:], op=ALU.add), reads=[b_bank[6], b_yt1[g]], writes=[b_yt1[g]])
                    else:
                        S.op("dve", lambda e, g=g: e.tensor_copy(out=yt1[:, g * 512:(g + 1) * 512], in_=banks[6][:]), reads=[b_bank[6]], writes=[b_yt1[g]])
                if p5_cut < 3:
                    continue
                S.op("dve", lambda e: e.tensor_tensor(out=yt2[:].rearrange("p (h d) -> p h d", h=32), in0=xc.rearrange("p (h d) -> p h d", h=32), in1=bc3(D_b[:], 64), op=ALU.mult), reads=[b_xin[bi], b_Db], writes=[b_yt2])
                S.op("pool", lambda e: e.tensor_tensor(out=yt2[:], in0=yt2[:], in1=yt1[:], op=ALU.add), reads=b_yt1 + [b_yt2], writes=[b_yt2])
                S.op("dve", lambda e: e.tensor_tensor(out=yt2[:], in0=yt2[:], in1=zc, op=ALU.mult), reads=[b_yt2, b_zin[bi]], writes=[b_yt2])
                for g in range(4):
                    S.op("act", lambda e, g=g: e.activation(out=sq5[:], in_=yt2[:, g * 512:(g + 1) * 512], func=AF.Square, accum_out=ss5[:, g:g + 1]), reads=[b_yt2], writes=[b_sq5, b_ss5])
                S.op("dve", lambda e: e.tensor_scalar(out=ss5[:], in0=ss5[:], scalar1=1.0 / 512, scalar2=1e-6, op0=ALU.mult, op1=ALU.add), reads=[b_ss5], writes=[b_ss5])
                S.op("act", lambda e: e.activation(out=ss5[:], in_=ss5[:], func=AF.Sqrt), reads=[b_ss5], writes=[b_ss5])
                S.op("dve", lambda e: e.reciprocal(out=ss5[:], in_=ss5[:]), reads=[b_ss5], writes=[b_ss5])
                for g in range(4):
                    S.op("dve", lambda e, g=g: e.scalar_tensor_tensor(out=osb[:, g * 512:(g + 1) * 512], in0=yt2[:, g * 512:(g + 1) * 512], scalar=ss5[:, g:g + 1], in1=nw_b[:, g * 512:(g + 1) * 512], op0=ALU.mult, op1=ALU.mult),
                         reads=[b_yt2, b_ss5, b_nwb], writes=[b_osb])
                if "dbg_ossm" in dbg:
                    d_ = nc.dram_tensor("dbg_ossm%d" % ci, [128, 2048], BF16, kind="ExternalOutput").ap()
                    final_evs.append(S.dma("sp", lambda e, d_=d_: e.dma_start(out=d_, in_=osb[:]), reads=[b_osb]))
                if p5_cut < 4:
                    continue
                for half in range(2):
                    tk = 6 + half
                    tpv = banks[tk].bitcast(BF16)
                    for kk in range(8):
                        kc = half * 8 + kk
                        S.op("pe", lambda e, kk=kk, kc=kc, tpv=tpv: e.transpose(out=tpv[:, kk * 128:(kk + 1) * 128], in_=osb[:, kc * 128:(kc + 1) * 128], identity=ident[:]), reads=[b_osb, b_ident], writes=[b_bank[tk]], skip_self=True)
                    S.op("act", lambda e, half=half, tpv=tpv: e.activation(out=osT[:, half * 8:(half + 1) * 8, osl], in_=tpv[:, :].rearrange("p (k t) -> p k t", k=8), func=AF.Copy), reads=[b_bank[tk]], writes=[b_osT])
                if p5_cut < 5:
                    continue
                if ci < NST - 1:
                    S.op("dve", lambda e: e.tensor_tensor(out=dout[:], in0=alast[:], in1=acum[:], op=ALU.subtract), reads=[b_alast, b_acum], writes=[b_dout])
                    S.op("act", lambda e: e.activation(out=dout[:], in_=dout[:], func=AF.Exp), reads=[b_dout], writes=[b_dout])
                    S.op("act", lambda e: e.activation(out=cdb[:], in_=alast[:], func=AF.Exp), reads=[b_alast], writes=[b_cdb])
                    S.op("dve", lambda e: e.tensor_tensor(out=xw[:].rearrange("p (h d) -> p h d", h=32), in0=xdt[:].rearrange("p (h d) -> p h d", h=32), in1=bc3(dout[:], 64), op=ALU.mult), reads=[b_xdt, b_dout], writes=[b_xw])
                    for g in range(4):
                        bk = 1 + g
                        S.op("pe", lambda e, g=g, bk=bk: e.matmul(banks[bk][:], lhsT=bmin[bi][:, g * 128:(g + 1) * 128], rhs=xw[:, g * 512:(g + 1) * 512], start=True, stop=True), reads=[b_bmin[bi], b_xw], writes=[b_bank[bk]])
                        if ci > 0 and not p5_nostate:
                            S.op("dve", lambda e, g=g: e.tensor_tensor(out=htmp[:].rearrange("p (h d) -> p h d", h=8), in0=HT[:, g * 512:(g + 1) * 512].rearrange("p (h d) -> p h d", h=8), in1=bc3(cdb[:, g * 8:(g + 1) * 8], 64), op=ALU.mult),
                                 reads=[b_HT[g], b_cdb], writes=[b_htmp])
                            S.op("dve", lambda e, g=g, bk=bk: e.tensor_tensor(out=HT[:, g * 512:(g + 1) * 512], in0=htmp[:], in1=banks[bk][:], op=ALU.add), reads=[b_htmp, b_bank[bk]], writes=[b_HT[g]])
                        else:
                            S.op("dve", lambda e, g=g, bk=bk: e.tensor_copy(out=HT[:, g * 512:(g + 1) * 512], in_=banks[bk][:]), reads=[b_bank[bk]], writes=[b_HT[g]])
                        S.op("act", lambda e, g=g: e.activation(out=HTb[:, g * 512:(g + 1) * 512], in_=HT[:, g * 512:(g + 1) * 512], func=AF.Copy), reads=[b_HT[g]], writes=[b_HTb[g]])
                if p5_cut < 6:
                    continue
                if j4 == 3:
                    for oc in range(8):
                        bk = 1 + (oc % 4)
                        for kc in range(16):
                            S.op("pe", lambda e, kc=kc, oc=oc, bk=bk: e.matmul(banks[bk][:], lhsT=wso[:, kc, oc * 128:(oc + 1) * 128], rhs=osT[:, kc, :], start=(kc == 0), stop=(kc == 15)),
                                 reads=[b_wso, b_osT], writes=[b_bank[bk]], skip_self=True)
                        yi = oc % 2
                        S.op("act", lambda e, bk=bk, yi=yi: e.activation(out=ystg5[yi][:], in_=banks[bk][:], func=AF.Copy), reads=[b_bank[bk]], writes=[b_ystg5[yi]])
                        S.dma("sp", lambda e, yi=yi, oc=oc, tt=tt: e.dma_start(out=ysT_d[oc, :, tt * 512:(tt + 1) * 512], in_=ystg5[yi][:]), reads=[b_ystg5[yi]], writes=[b_ysT_d[tt]])
            S.barrier()

        if stop_after <= 5:
            S.wait_all("sp", final_evs)
            S.emit()
            return nc

        gtb = T("gtb", [128, 2, D], F32); b_gtb = Buf()
        nf_b = T("nf_b", [128, D], F32); b_nfb = Buf()
        S.dma("sp", lambda e: e.dma_start(out=gtb[:], in_=gtb_d.rearrange("j p n -> p j n")), reads=[b_gtb_d], writes=[b_gtb])
        S.dma("sp", lambda e: e.dma_start(out=nf_b[:], in_=I["norm_f_w"].partition_broadcast(128)), writes=[b_nfb])
        b_x1_d = [Buf() for _ in range(NST)]
        b_actT_d = [Buf() for _ in range(NT)]
        ph2 = st.enter_context(ExitStack())
        h2T = T("h2T", [128, 8, S_LEN], BF16, ph2); b_h2T = [Buf() for _ in range(NST)]
        sh2c = modT[:, 24:32]
        with ExitStack() as p6a:
            wo = T("p6wo", [128, 8, D], BF16, p6a); b_wo = Buf()
            fs6 = [T("p6fs%d" % i, [128, 8, 256], F32, p6a) for i in range(2)]; b_fs6 = [Buf(), Buf()]
            wo_v = I["w_o"].rearrange("(kc p) n -> p kc n", p=128)
            for i in range(4):
                S.dma("sp", lambda e, i=i: e.dma_start(out=fs6[i % 2][:], in_=wo_v[:, :, i * 256:(i + 1) * 256]), writes=[b_fs6[i % 2]])
                S.op("pool", lambda e, i=i: e.tensor_copy(out=wo[:, :, i * 256:(i + 1) * 256], in_=fs6[i % 2][:]), reads=[b_fs6[i % 2]], writes=[b_wo])
            ya = [T("p6ya%d" % i, [128, 8, 256], BF16, p6a) for i in range(2)]; b_ya = [Buf(), Buf()]
            ys = [T("p6ys%d" % i, [128, 8, 256], BF16, p6a) for i in range(2)]; b_ys = [Buf(), Buf()]
            gg = [T("p6gg%d" % i, [128, 16, 256], BF16, p6a) for i in range(2)]; b_gg = [Buf(), Buf()]
            xx = [T("p6xx%d" % i, [128, 2, D], F32, p6a) for i in range(2)]; b_xx = [Buf(), Buf()]
            m1 = T("p6m1", [128, 8, 256], F32, p6a); b_m1 = Buf()
            m2 = T("p6m2", [128, 8, 256], F32, p6a); b_m2 = Buf()
            mrg = T("p6mrg", [128, 8, 256], BF16, p6a); b_mrg = Buf()
            tmp6 = T("p6tmp", [128, 512], F32, p6a); b_tmp6 = Buf()
            junk6 = T("p6junk", [128, D], BF16, p6a); b_junk6 = Buf()
            ss6 = T("p6ss", [128, NST], F32, p6a); b_ss6 = [Buf() for _ in range(NST)]
            xn6 = [T("p6xn%d" % i, [128, D], BF16, p6a) for i in range(2)]; b_xn6 = [Buf(), Buf()]
            for hf in range(2 * NT):
                bi = hf % 2; r0 = hf * 256; tt = hf // 2
                S.dma("sp", lambda e, bi=bi, r0=r0: e.dma_start(out=ya[bi][:], in_=yaT_d[:, :, r0:r0 + 256].rearrange("c p t -> p c t")), reads=[b_yaT_d[tt]], writes=[b_ya[bi]])
                S.dma("sp", lambda e, bi=bi, r0=r0: e.dma_start(out=ys[bi][:], in_=ysT_d[:, :, r0:r0 + 256].rearrange("c p t -> p c t")), reads=[b_ysT_d[tt]], writes=[b_ys[bi]])
                for gh in range(2):
                    S.dma("sp", lambda e, bi=bi, r0=r0, gh=gh: e.dma_start(out=gg[bi][:, gh * 8:(gh + 1) * 8, :], in_=gT_d[gh * 8:(gh + 1) * 8, :, r0:r0 + 256].rearrange("c p t -> p c t")), reads=[b_gT_d[tt]], writes=[b_gg[bi]])
                S.dma("sp", lambda e, bi=bi, r0=r0: e.dma_start(out=xx[bi][:], in_=I["x"][r0:r0 + 256, :].rearrange("(j p) n -> p j n", p=128)), writes=[b_xx[bi]])
                S.op("pool", lambda e, bi=bi: e.tensor_tensor(out=m1[:], in0=gg[bi][:, 0:8, :], in1=ya[bi][:], op=ALU.mult), reads=[b_gg[bi], b_ya[bi]], writes=[b_m1])
                S.op("dve", lambda e, bi=bi: e.tensor_tensor(out=m2[:], in0=gg[bi][:, 8:16, :], in1=ys[bi][:], op=ALU.mult), reads=[b_gg[bi], b_ys[bi]], writes=[b_m2])
                S.op("dve", lambda e: e.tensor_tensor(out=mrg[:], in0=m1[:], in1=m2[:], op=ALU.add), reads=[b_m1, b_m2], writes=[b_mrg])
                for j2 in range(2):
                    sti = hf * 2 + j2
                    for half in range(2):
                        bk = half
                        for kc in range(8):
                            S.op("pe", lambda e, kc=kc, j2=j2, half=half, bk=bk: e.matmul(banks[bk][:], lhsT=mrg[:, kc, j2 * 128:(j2 + 1) * 128], rhs=wo[:, kc, half * 512:(half + 1) * 512], start=(kc == 0), stop=(kc == 7)),
                                 reads=[b_mrg, b_wo], writes=[b_bank[bk]], skip_self=True)
                        S.op("dve", lambda e, half=half, bk=bk: e.tensor_tensor(out=tmp6[:], in0=banks[bk][:], in1=gtb[:, 0, half * 512:(half + 1) * 512], op=ALU.mult), reads=[b_bank[bk], b_gtb], writes=[b_tmp6])
                        S.op("dve", lambda e, half=half, bi=bi, j2=j2: e.tensor_tensor(out=xx[bi][:, j2, half * 512:(half + 1) * 512], in0=xx[bi][:, j2, half * 512:(half + 1) * 512], in1=tmp6[:], op=ALU.add), reads=[b_tmp6, b_xx[bi]], writes=[b_xx[bi]])
                    S.dma("sp", lambda e, bi=bi, j2=j2, sti=sti: e.dma_start(out=x1_d[sti * 128:(sti + 1) * 128, :], in_=xx[bi][:, j2, :]), reads=[b_xx[bi]], writes=[b_x1_d[sti]])
                    S.op("act", lambda e, bi=bi, j2=j2, sti=sti: e.activation(out=junk6[:], in_=xx[bi][:, j2, :], func=AF.Square, accum_out=ss6[:, sti:sti + 1]), reads=[b_xx[bi]], writes=[b_junk6, b_ss6[sti]])
                    S.op("dve", lambda e, sti=sti: e.tensor_scalar(out=ss6[:, sti:sti + 1], in0=ss6[:, sti:sti + 1], scalar1=1.0 / D, scalar2=1e-6, op0=ALU.mult, op1=ALU.add), reads=[b_ss6[sti]], writes=[b_ss6[sti]])
                    S.op("act", lambda e, sti=sti: e.activation(out=ss6[:, sti:sti + 1], in_=ss6[:, sti:sti + 1], func=AF.Sqrt), reads=[b_ss6[sti]], writes=[b_ss6[sti]])
                    S.op("dve", lambda e, sti=sti: e.reciprocal(out=ss6[:, sti:sti + 1], in_=ss6[:, sti:sti + 1]), reads=[b_ss6[sti]], writes=[b_ss6[sti]])
                    xi = sti % 2
                    S.op("dve", lambda e, sti=sti, xi=xi, bi=bi, j2=j2: e.tensor_scalar(out=xn6[xi][:], in0=xx[bi][:, j2, :], scalar1=ss6[:, sti:sti + 1], scalar2=None, op0=ALU.mult), reads=[b_xx[bi], b_ss6[sti]], writes=[b_xn6[xi]])
                    tk = 2 + xi
                    tpv = banks[tk].bitcast(BF16)
                    for c in range(8):
                        S.op("pe", lambda e, c=c, xi=xi, tpv=tpv: e.transpose(out=tpv[:, c * 128:(c + 1) * 128], in_=xn6[xi][:, c * 128:(c + 1) * 128], identity=ident[:]), reads=[b_xn6[xi], b_ident], writes=[b_bank[tk]], skip_self=True)
                    for c in range(8):
                        S.op("act", lambda e, c=c, sti=sti, tpv=tpv: e.activation(out=h2T[:, c, sti * 128:(sti + 1) * 128], in_=tpv[:, c * 128:(c + 1) * 128], func=AF.Identity, scale=A2[:, c:c + 1], bias=sh2c[:, c:c + 1]),
                             reads=[b_bank[tk], b_A2, b_modT], writes=[b_h2T[sti]])
            S.barrier()

        if stop_after <= 6:
            S.wait_all("sp", final_evs)
            S.emit()
            return nc

        with ExitStack() as p6b:
            wg_v = I["w_gate"].rearrange("(kc p) n -> p kc n", p=128)
            wu_v = I["w_up"].rearrange("(kc p) n -> p kc n", p=128)
            load6 = make_wloader(p6b, wg_v, nw=3, nstg=2, tag="p6b")
            sg = [T("p6sg%d" % i, [128, 512], BF16, p6b) for i in range(2)]; b_sg = [Buf(), Buf()]
            ast = [T("p6ast%d" % i, [128, 512], BF16, p6b) for i in range(3)]; b_ast = [Buf() for _ in range(3)]
            ctr6 = [0]
            groups6 = []
            for c0 in range(0, DFF, 256):
                groups6.append(c0)
            cur = load6([(groups6[0], 256, [0], wg_v), (groups6[0], 256, [256], wu_v)])
            for gi6, c0 in enumerate(groups6):
                nxt = load6([(groups6[gi6 + 1], 256, [0], wg_v), (groups6[gi6 + 1], 256, [256], wu_v)]) if gi6 + 1 < len(groups6) else None
                w, bw = cur
                for cc in range(2):
                    c = c0 // 128 + cc
                    for tt in range(NT):
                        kg = 0 + (ctr6[0] % 2); ku = 2 + (ctr6[0] % 2); si = ctr6[0] % 2; ai = ctr6[0] % 3; ctr6[0] += 1
                        for kc in range(8):
                            S.op("pe", lambda e, kc=kc, kg=kg, cc=cc, tt=tt, w=w: e.matmul(banks[kg][:], lhsT=w[:, kc, cc * 128:(cc + 1) * 128], rhs=h2T[:, kc, tt * 512:(tt + 1) * 512], start=(kc == 0), stop=(kc == 7)),
                                 reads=[bw] + b_h2T[tt * 4:tt * 4 + 4], writes=[b_bank[kg]], skip_self=True)
                        for kc in range(8):
                            S.op("pe", lambda e, kc=kc, ku=ku, cc=cc, tt=tt, w=w: e.matmul(banks[ku][:], lhsT=w[:, kc, 256 + cc * 128:256 + (cc + 1) * 128], rhs=h2T[:, kc, tt * 512:(tt + 1) * 512], start=(kc == 0), stop=(kc == 7)),
                                 reads=[bw] + b_h2T[tt * 4:tt * 4 + 4], writes=[b_bank[ku]], skip_self=True)
                        S.op("act", lambda e, kg=kg, si=si: e.activation(out=sg[si][:], in_=banks[kg][:], func=AF.Silu), reads=[b_bank[kg]], writes=[b_sg[si]])
                        S.op("dve", lambda e, ku=ku, si=si, ai=ai: e.tensor_tensor(out=ast[ai][:], in0=sg[si][:], in1=banks[ku][:], op=ALU.mult), reads=[b_sg[si], b_bank[ku]], writes=[b_ast[ai]])
                        S.dma("sp", lambda e, ai=ai, c=c, tt=tt: e.dma_start(out=actT_d[c, :, tt * 512:(tt + 1) * 512], in_=ast[ai][:]), reads=[b_ast[ai]], writes=[b_actT_d[tt]])
                cur = nxt
            S.barrier()
        ph2.close()

        with ExitStack() as p6c:
            wdn = T("p6wdn", [128, 22, D], BF16, p6c); b_wdn = Buf()
            fs7 = [T("p6cfs%d" % i, [128, 8, 256], F32, p6c) for i in range(2)]; b_fs7 = [Buf(), Buf()]
            wd_v = I["w_down"].rearrange("(kc p) n -> p kc n", p=128)
            ii = 0
            for (k0, k1) in ((0, 8), (8, 16), (16, 22)):
                for i in range(4):
                    S.dma("sp", lambda e, i=i, k0=k0, k1=k1, ii=ii: e.dma_start(out=fs7[ii % 2][:, 0:k1 - k0, :], in_=wd_v[:, k0:k1, i * 256:(i + 1) * 256]), writes=[b_fs7[ii % 2]])
                    S.op("pool", lambda e, i=i, k0=k0, k1=k1, ii=ii: e.tensor_copy(out=wdn[:, k0:k1, i * 256:(i + 1) * 256], in_=fs7[ii % 2][:, 0:k1 - k0, :]), reads=[b_fs7[ii % 2]], writes=[b_wdn])
                    ii += 1
            at = [T("p6at%d" % i, [128, 22, 256], BF16, p6c) for i in range(2)]; b_at = [Buf(), Buf()]
            x1t = [T("p6x1t%d" % i, [128, 2, D], F32, p6c) for i in range(2)]; b_x1t = [Buf(), Buf()]
            tmp7 = T("p6ctmp", [128, 512], F32, p6c); b_tmp7 = Buf()
            junk7 = T("p6cjunk", [128, D], BF16, p6c); b_junk7 = Buf()
            ss7 = T("p6css", [128, NST], F32, p6c); b_ss7 = [Buf() for _ in range(NST)]
            ot = [T("p6ot%d" % i, [128, D], F32, p6c) for i in range(2)]; b_ot = [Buf(), Buf()]
            for hf in range(2 * NT):
                bi = hf % 2; r0 = hf * 256; tt = hf // 2
                for ah in range(0, 22, 6):
                    ae = min(22, ah + 6)
                    S.dma("sp", lambda e, bi=bi, r0=r0, ah=ah, ae=ae: e.dma_start(out=at[bi][:, ah:ae, :], in_=actT_d[ah:ae, :, r0:r0 + 256].rearrange("c p t -> p c t")), reads=[b_actT_d[tt]], writes=[b_at[bi]])
                S.dma("sp", lambda e, bi=bi, r0=r0: e.dma_start(out=x1t[bi][:], in_=x1_d[r0:r0 + 256, :].rearrange("(j p) n -> p j n", p=128)), reads=[b_x1_d[hf * 2], b_x1_d[hf * 2 + 1]], writes=[b_x1t[bi]])
                for j2 in range(2):
                    sti = hf * 2 + j2
                    for half in range(2):
                        bk = 4 + half
                        for c in range(22):
                            S.op("pe", lambda e, c=c, j2=j2, half=half, bk=bk, bi=bi: e.matmul(banks[bk][:], lhsT=at[bi][:, c, j2 * 128:(j2 + 1) * 128], rhs=wdn[:, c, half * 512:(half + 1) * 512], start=(c == 0), stop=(c == 21)),
                                 reads=[b_at[bi], b_wdn], writes=[b_bank[bk]], skip_self=True)
                        S.op("dve", lambda e, half=half, bk=bk: e.tensor_tensor(out=tmp7[:], in0=banks[bk][:], in1=gtb[:, 1, half * 512:(half + 1) * 512], op=ALU.mult), reads=[b_bank[bk], b_gtb], writes=[b_tmp7])
                        S.op("dve", lambda e, half=half, bi=bi, j2=j2: e.tensor_tensor(out=x1t[bi][:, j2, half * 512:(half + 1) * 512], in0=x1t[bi][:, j2, half * 512:(half + 1) * 512], in1=tmp7[:], op=ALU.add), reads=[b_tmp7, b_x1t[bi]], writes=[b_x1t[bi]])
                    S.op("act", lambda e, bi=bi, j2=j2, sti=sti: e.activation(out=junk7[:], in_=x1t[bi][:, j2, :], func=AF.Square, accum_out=ss7[:, sti:sti + 1]), reads=[b_x1t[bi]], writes=[b_junk7, b_ss7[sti]])
                    S.op("dve", lambda e, sti=sti: e.tensor_scalar(out=ss7[:, sti:sti + 1], in0=ss7[:, sti:sti + 1], scalar1=1.0 / D, scalar2=1e-6, op0=ALU.mult, op1=ALU.add), reads=[b_ss7[sti]], writes=[b_ss7[sti]])
                    S.op("act", lambda e, sti=sti: e.activation(out=ss7[:, sti:sti + 1], in_=ss7[:, sti:sti + 1], func=AF.Sqrt), reads=[b_ss7[sti]], writes=[b_ss7[sti]])
                    S.op("dve", lambda e, sti=sti: e.reciprocal(out=ss7[:, sti:sti + 1], in_=ss7[:, sti:sti + 1]), reads=[b_ss7[sti]], writes=[b_ss7[sti]])
                    oi = sti % 2
                    S.op("dve", lambda e, sti=sti, oi=oi, bi=bi, j2=j2: e.scalar_tensor_tensor(out=ot[oi][:], in0=x1t[bi][:, j2, :], scalar=ss7[:, sti:sti + 1], in1=nf_b[:], op0=ALU.mult, op1=ALU.mult), reads=[b_x1t[bi], b_ss7[sti], b_nfb], writes=[b_ot[oi]])
                    final_evs.append(S.dma("sp", lambda e, oi=oi, sti=sti: e.dma_start(out=out_d[sti * 128:(sti + 1) * 128, :], in_=ot[oi][:]), reads=[b_ot[oi]]))
            S.barrier()

        S.wait_all("sp", final_evs)
        S.emit()
    return nc


def make_inputs(inputs, b):
    f = lambda a: np.ascontiguousarray(np.asarray(a, dtype=np.float32))
    col8 = lambda v: f(np.asarray(v).reshape(8, 128).T)
    m = {
        "x": f(inputs["x"][b]), "c_col": col8(inputs["c"][b]), "w_ada": f(inputs["w_ada"][0]),
        "b_ada_col": f(np.asarray(inputs["b_ada"][0]).reshape(48, 128).T), "b_ada_row": f(np.asarray(inputs["b_ada"][0])[None, :]),
        "norm1_col": col8(inputs["norm1_w"][0]), "w_in": f(inputs["w_in"][0]),
        "cmp_pe_k": f(inputs["cmp_pe_k"][0]), "cmp_w1_k": f(inputs["cmp_w1_k"][0]), "cmp_w2_k": f(inputs["cmp_w2_k"][0]),
        "cmp_pe_v": f(inputs["cmp_pe_v"][0]), "cmp_w1_v": f(inputs["cmp_w1_v"][0]), "cmp_w2_v": f(inputs["cmp_w2_v"][0]),
        "conv_w_col": f(np.asarray(inputs["conv_w"][0]).T.reshape(24, 128, 4).transpose(1, 0, 2)),
        "conv_b_col": f(np.asarray(inputs["conv_b"][0]).reshape(24, 128).T),
        "dt_bias": f(np.asarray(inputs["dt_bias"][0])[None, :]), "a_log": f(np.asarray(inputs["a_log"][0])[None, :]),
        "d_skip": f(np.asarray(inputs["d_skip"][0])[None, :]), "ssm_norm_w": f(np.asarray(inputs["ssm_norm_w"][0])[None, :]),
        "w_attn_out": f(inputs["w_attn_out"][0]), "w_ssm_out": f(inputs["w_ssm_out"][0]), "w_o": f(inputs["w_o"][0]),
        "norm2_col": col8(inputs["norm2_w"][0]), "w_gate": f(inputs["w_gate"][0]), "w_up": f(inputs["w_up"][0]),
        "w_down": f(inputs["w_down"][0]), "norm_f_w": f(np.asarray(inputs["norm_f_w"])[None, :]),
    }
    return m


def kernel(**inputs):
    nc = build()
    consts = host_consts()
    in_maps = []
    shared = None
    for b in range(8):
        m = make_inputs(inputs, b)
        if shared is None:
            shared = {k: v for k, v in m.items() if k not in ("x", "c_col")}
        else:
            for k in shared:
                m[k] = shared[k]
        m.update(consts)
        in_maps.append(m)
    res = run_bass_kernel_spmd(nc, in_maps, core_ids=list(range(8)))
    out = np.stack([np.asarray(r["out"], dtype=np.float32) for r in res.results], axis=0)
    return out
```
